# Optimizing a Trainium2 kernel written in Bass

```python
import jax, jax.numpy as jnp
from jax import lax
import numpy as np

D_MODEL = 1024
BATCH = 32
SEQ = 2048
DEPTH = 2
DEC_BATCH = 8
DEC_SEQ = 16
PAST_LEN = 1024

CHUNK = 64
D_MIX = D_MODEL
D_POOL = D_MIX // 4
D_GMLP = (D_MIX - D_POOL) // 2
D_CONV = D_MIX - D_POOL - D_GMLP
POOL_WINDOWS = (2, 4, 8, 16)
N_POOL_GROUPS = len(POOL_WINDOWS)
POOL_GROUP = D_POOL // N_POOL_GROUPS
POOL_CTX = max(POOL_WINDOWS) - 1
GMLP_CHUNK = 128
GMLP_HEADS = 4
GMLP_HEAD_DIM = D_GMLP // GMLP_HEADS
CONV_WIDTH = 31
CONV_CTX = CONV_WIDTH - 1
COL_SIZES = (D_POOL, D_POOL, D_GMLP, D_GMLP, D_GMLP, D_CONV, D_CONV, D_CONV)
D_IN = sum(COL_SIZES)
SPLITS = tuple(int(s) for s in np.cumsum(COL_SIZES)[:-1])
ALPHA = (2 * DEPTH) ** 0.25
BETA = (8 * DEPTH) ** -0.25
LN_EPS = 1e-5

kernel_name = 'hybrid_pool_gmlp_conformer_stream_step'


def layer_norm(x, g, b):
    xf = x.astype(jnp.float32)
    mu = jnp.mean(xf, axis=-1, keepdims=True)
    xc = xf - mu
    var = jnp.mean(xc * xc, axis=-1, keepdims=True)
    y = xc * lax.rsqrt(var + LN_EPS) * g.astype(jnp.float32) + b.astype(jnp.float32)
    return y.astype(x.dtype)


def pool_mixer(ext, a, pos0, pool_w, pool_scale):
    B, T, _ = a.shape
    cs = jnp.cumsum(ext.astype(jnp.float32), axis=1)
    cs = jnp.pad(cs, ((0, 0), (1, 0), (0, 0)))
    pos = pos0 + jnp.arange(T)
    af = a.astype(jnp.float32)
    diffs = []
    for g, w in enumerate(POOL_WINDOWS):
        sl = slice(g * POOL_GROUP, (g + 1) * POOL_GROUP)
        s = cs[:, POOL_CTX + 1:POOL_CTX + 1 + T, sl] - cs[:, POOL_CTX + 1 - w:POOL_CTX + 1 - w + T, sl]
        cnt = jnp.minimum(w, pos + 1).astype(jnp.float32)
        diffs.append(s / cnt[None, :, None] - af[..., sl])
    d = jnp.stack(diffs, axis=2)
    y = jnp.einsum('btgc,gce->btge', d, pool_w.astype(jnp.float32)).reshape(B, T, D_POOL)
    return (y * pool_scale.astype(jnp.float32)).astype(a.dtype)


def spatial_mix(v, w_s, b_s):
    B, T, _ = v.shape
    nc = -(-T // GMLP_CHUNK)
    pad = nc * GMLP_CHUNK - T
    vp = jnp.pad(v, ((0, 0), (0, pad), (0, 0))).reshape(B, nc, GMLP_CHUNK, GMLP_HEADS, GMLP_HEAD_DIM)
    mask = jnp.tril(jnp.ones((GMLP_CHUNK, GMLP_CHUNK), dtype=bool))
    wm = jnp.where(mask[None], w_s, jnp.zeros_like(w_s))
    z = jnp.einsum('hts,bnshc->bnthc', wm, vp) + jnp.transpose(b_s)[:, :, None]
    return z.reshape(B, nc * GMLP_CHUNK, D_GMLP)[:, :T]


def conformer_tail(ext, conv_w, conv_b, ln_g, ln_b, pw_w, pw_b):
    y = lax.conv_general_dilated(ext, conv_w[:, None, :].astype(ext.dtype), (1,), 'VALID',
                                 dimension_numbers=('NWC', 'WIO', 'NWC'),
                                 feature_group_count=D_CONV) + conv_b
    y = jax.nn.silu(layer_norm(y, ln_g, ln_b))
    return jnp.einsum('btc,ce->bte', y, pw_w) + pw_b


def mixer_layer(x, pool_left, conv_left, pos0, w_in, pool_w, pool_scale, gmlp_ln_g, gmlp_ln_b,
                gmlp_w, gmlp_b, conv_w, conv_b, conv_ln_g, conv_ln_b, conv_pw_w, conv_pw_b,
                w_out, b_out, ln_g, ln_b):
    h = jnp.einsum('btd,de->bte', x, w_in)
    a, g_a, u, v, g_b, c1, c2, g_c = jnp.split(h, SPLITS, axis=-1)
    pool_ext = jnp.concatenate([pool_left.astype(a.dtype), a], axis=1)
    y_a = pool_mixer(pool_ext, a, pos0, pool_w, pool_scale)
    v_n = layer_norm(v, gmlp_ln_g, gmlp_ln_b)
    y_b = u * spatial_mix(v_n, gmlp_w, gmlp_b)
    c = c1 * jax.nn.sigmoid(c2)
    conv_ext = jnp.concatenate([conv_left.astype(c.dtype), c], axis=1)
    y_c = conformer_tail(conv_ext, conv_w, conv_b, conv_ln_g, conv_ln_b, conv_pw_w, conv_pw_b)
    mixed = jnp.concatenate([jax.nn.silu(g_a) * y_a, jax.nn.silu(g_b) * y_b,
                             jax.nn.silu(g_c) * y_c], axis=-1)
    out = jnp.einsum('bte,ed->btd', mixed, w_out) + b_out
    x_new = layer_norm(ALPHA * x + out, ln_g, ln_b)
    return x_new, pool_ext[:, -POOL_CTX:], conv_ext[:, -CONV_CTX:], v_n


def setup_inputs(seed: int = 0) -> dict:
    key = jax.random.key(seed)
    ks = jax.random.split(key, 24)
    f32 = jnp.float32
    nrm = lambda k, s, sc: (jax.random.normal(k, s, f32) * sc)
    return {
        'x_prompt': nrm(ks[0], (BATCH, SEQ, D_MODEL), 1.0),
        'x_sample': nrm(ks[1], (DEC_BATCH, DEC_SEQ, D_MODEL), 1.0),
        'state_pool': nrm(ks[2], (DEPTH, DEC_BATCH, POOL_CTX, D_POOL), 1.0),
        'state_conv': nrm(ks[3], (DEPTH, DEC_BATCH, CONV_CTX, D_CONV), 0.5),
        'w_in': nrm(ks[4], (DEPTH, D_MODEL, D_IN), D_MODEL ** -0.5),
        'pool_w': nrm(ks[5], (DEPTH, N_POOL_GROUPS, POOL_GROUP, POOL_GROUP), POOL_GROUP ** -0.5),
        'pool_scale': 1.0 + nrm(ks[6], (DEPTH, D_POOL), 0.02),
        'gmlp_ln_g': 1.0 + nrm(ks[7], (DEPTH, D_GMLP), 0.02),
        'gmlp_ln_b': nrm(ks[8], (DEPTH, D_GMLP), 0.02),
        'gmlp_w': nrm(ks[9], (DEPTH, GMLP_HEADS, GMLP_CHUNK, GMLP_CHUNK), GMLP_CHUNK ** -0.5),
        'gmlp_b': 1.0 + nrm(ks[10], (DEPTH, GMLP_HEADS, GMLP_CHUNK), 0.02),
        'conv_w': nrm(ks[11], (DEPTH, CONV_WIDTH, D_CONV), CONV_WIDTH ** -0.5),
        'conv_b': nrm(ks[12], (DEPTH, D_CONV), 0.02),
        'conv_ln_g': 1.0 + nrm(ks[13], (DEPTH, D_CONV), 0.02),
        'conv_ln_b': nrm(ks[14], (DEPTH, D_CONV), 0.02),
        'conv_pw_w': nrm(ks[15], (DEPTH, D_CONV, D_CONV), D_CONV ** -0.5),
        'conv_pw_b': nrm(ks[16], (DEPTH, D_CONV), 0.02),
        'w_out': nrm(ks[17], (DEPTH, D_MIX, D_MODEL), BETA * D_MIX ** -0.5),
        'b_out': nrm(ks[18], (DEPTH, D_MODEL), 0.02),
        'ln_g': 1.0 + nrm(ks[19], (DEPTH, D_MODEL), 0.02),
        'ln_b': nrm(ks[20], (DEPTH, D_MODEL), 0.02),
    }


def reference(x_prompt, x_sample, state_pool, state_conv, w_in, pool_w, pool_scale, gmlp_ln_g,
              gmlp_ln_b, gmlp_w, gmlp_b, conv_w, conv_b, conv_ln_g, conv_ln_b, conv_pw_w,
              conv_pw_b, w_out, b_out, ln_g, ln_b):
    Bp = x_prompt.shape[0]
    hp, hs = x_prompt, x_sample
    pool_p, conv_p, pool_s, conv_s, v_s = [], [], [], [], []
    for l in range(DEPTH):
        params = (w_in[l], pool_w[l], pool_scale[l], gmlp_ln_g[l], gmlp_ln_b[l], gmlp_w[l], gmlp_b[l],
                  conv_w[l], conv_b[l], conv_ln_g[l], conv_ln_b[l], conv_pw_w[l], conv_pw_b[l],
                  w_out[l], b_out[l], ln_g[l], ln_b[l])
        zp = jnp.zeros((Bp, POOL_CTX, D_POOL), hp.dtype)
        zc = jnp.zeros((Bp, CONV_CTX, D_CONV), hp.dtype)
        hp, np_pool, np_conv, _ = mixer_layer(hp, zp, zc, 0, *params)
        pool_p.append(np_pool)
        conv_p.append(np_conv)
        hs, ns_pool, ns_conv, ns_v = mixer_layer(hs, state_pool[l], state_conv[l], PAST_LEN, *params)
        pool_s.append(ns_pool)
        conv_s.append(ns_conv)
        v_s.append(ns_v)
    new_pool_prompt = jnp.stack(pool_p)
    new_conv_prompt = jnp.stack(conv_p)
    new_pool_sample = jnp.stack(pool_s)
    new_conv_sample = jnp.stack(conv_s)
    new_gmlp_v_sample = jnp.stack(v_s)
    return (hp, hs, new_pool_prompt, new_conv_prompt, new_pool_sample, new_conv_sample, new_gmlp_v_sample)
```

```python
import contextlib
import os
import sys
import numpy as np
import concourse.bass as bass
import concourse.mybir as mybir
from concourse.bass_utils import run_bass_kernel_spmd

F32 = mybir.dt.float32
BF16 = mybir.dt.bfloat16
I32 = mybir.dt.int32
AF = mybir.ActivationFunctionType
ALU = mybir.AluOpType

D_MODEL = 1024
DEPTH = 2
D_POOL = 256
D_GMLP = 384
D_CONV = 384
D_IN = 2816
POOL_WINDOWS = (2, 4, 8, 16)
CONV_W = 31
ALPHA = float((2 * DEPTH) ** 0.25)
LN_EPS = 1e-5
NCORES = 8

COLS = dict(a=(0, 256), ga=(256, 256), u=(512, 384), v=(896, 384), gb=(1280, 384),
            c1=(1664, 384), c2=(2048, 384), gc=(2432, 384))
GROUPS = [('v', 896, 384), ('c2', 2048, 384), ('c1', 1664, 384), ('a', 0, 256), ('ga', 256, 256),
          ('gb', 1280, 384), ('u', 512, 384), ('gc', 2432, 384)]
GOFF = {}
_o = 0
for _n, _c, _w in GROUPS:
    GOFF[_n] = _o
    _o += _w
NG = len(GROUPS)
NRING = 3
ALL_HARD = True


class Sched:
    ENG = ('pe', 'act', 'dve', 'pool', 'sp')

    def __init__(self):
        self.ops = []
        self.force_hard = False
        self.defer = False
        self.deferred = []

    def flush(self, k=None):
        k = len(self.deferred) if k is None else min(k, len(self.deferred))
        self.ops.extend(self.deferred[:k])
        del self.deferred[:k]

    def op(self, eng, fn, reads=(), writes=(), chan=None, hard=False, partial=False):
        hard = hard or self.force_hard or (ALL_HARD and eng != 'pe')
        tgt = self.deferred if self.defer else self.ops
        tgt.append(dict(eng=eng, fn=fn, reads=tuple(reads), writes=tuple(writes), chan=chan,
                             hard=hard, partial=partial, idx=-1, line=sys._getframe(1).f_lineno))

    def analyze(self):
        assert not self.deferred
        for i, o in enumerate(self.ops):
            o['idx'] = i
        writers = {}
        readers = {}
        for o in self.ops:
            deps = set()
            for r in o['reads']:
                deps.update(writers.get(r, ()))
            for w in o['writes']:
                if not o['partial']:
                    deps.update(writers.get(w, ()))
                elif writers.get(w) and not self.ops[writers[w][0]]['partial']:
                    deps.add(writers[w][0])
                deps.update(readers.get(w, ()))
            deps.discard(o['idx'])
            o['deps'] = deps
            for r in o['reads']:
                readers.setdefault(r, []).append(o['idx'])
            for w in o['writes']:
                if o['partial']:
                    writers.setdefault(w, []).append(o['idx'])
                else:
                    writers[w] = [o['idx']]
                readers[w] = []
        need = set()
        for o in self.ops:
            for d in o['deps']:
                do = self.ops[d]
                if do['chan'] is not None or do['eng'] != o['eng'] or do['hard']:
                    need.add(d)
        cnt = {e: 0 for e in self.ENG}
        ccnt = {}
        for o in self.ops:
            if o['chan'] is not None:
                ccnt[o['chan']] = ccnt.get(o['chan'], 0) + 16
                o['sig'] = ('c', o['chan'], ccnt[o['chan']])
            elif o['idx'] in need:
                cnt[o['eng']] += 1
                o['sig'] = ('e', o['eng'], cnt[o['eng']])
            else:
                o['sig'] = None
        self.chan_total = ccnt
        self.eng_total = cnt


def run_sched(nc, S, final_wait_eng='sp'):
    S.analyze()
    with contextlib.ExitStack() as st:
        sem_e = {e: st.enter_context(nc.semaphore('s_' + e)) for e in S.ENG}
        sem_c = {c: st.enter_context(nc.semaphore('c_' + str(c))) for c in S.chan_total}
        block = st.enter_context(nc.Block())

        def semof(sig):
            return sem_e[sig[1]] if sig[0] == 'e' else sem_c[sig[1]]

        def gen(engname, engobj):
            waited = {}
            for o in S.ops:
                if o['eng'] != engname:
                    continue
                need_w = {}
                for d in o['deps']:
                    do = S.ops[d]
                    if do['chan'] is None and do['eng'] == engname and not do['hard']:
                        continue
                    sig = do['sig']
                    key = (sig[0], sig[1])
                    if sig[2] > need_w.get(key, 0):
                        need_w[key] = sig[2]
                for key in sorted(need_w):
                    if waited.get(key, 0) >= need_w[key]:
                        continue
                    engobj.wait_ge(semof((key[0], key[1], 0)), need_w[key])
                    waited[key] = need_w[key]
                ins = o['fn'](engobj)
                if o['sig'] is not None:
                    ins.then_inc(semof(o['sig']), 16 if o['sig'][0] == 'c' else 1)
            if engname == final_wait_eng:
                for c, tot in S.chan_total.items():
                    if waited.get(('c', c), 0) < tot:
                        engobj.wait_ge(sem_c[c], tot)
                for e, tot in S.eng_total.items():
                    if e != engname and tot > 0 and waited.get(('e', e), 0) < tot:
                        engobj.wait_ge(sem_e[e], tot)

        @block.tensor
        def _(e):
            gen('pe', e)

        @block.scalar
        def _(e):
            gen('act', e)

        @block.vector
        def _(e):
            gen('dve', e)

        @block.gpsimd
        def _(e):
            gen('pool', e)

        @block.sync
        def _(e):
            gen('sp', e)


def build_program(nseq, S_len, T=512):
    assert S_len % T == 0
    ntile = S_len // T
    nc = bass.Bass("TRN2", target_bir_lowering=False)
    dt_in = lambda name, shape: nc.dram_tensor(name, list(shape), F32, kind="ExternalInput").ap()
    dt_out = lambda name, shape: nc.dram_tensor(name, list(shape), F32, kind="ExternalOutput").ap()
    xp = dt_in("xp", [nseq, S_len, D_MODEL])
    xs = dt_in("xs", [16, D_MODEL])
    st_pool = dt_in("st_pool", [DEPTH, 15, D_POOL])
    st_conv = dt_in("st_conv", [DEPTH, 30, D_CONV])
    w_in = dt_in("w_in", [DEPTH, D_MODEL, D_IN])
    pool_w = dt_in("pool_w", [DEPTH, 4, 64, 64])
    pool_scale = dt_in("pool_scale", [DEPTH, D_POOL])
    gmlp_ln_g = dt_in("gmlp_ln_g", [DEPTH, D_GMLP])
    gmlp_ln_b = dt_in("gmlp_ln_b", [DEPTH, D_GMLP])
    gmlp_w = dt_in("gmlp_w", [DEPTH, 4, 128, 128])
    gmlp_b = dt_in("gmlp_b", [DEPTH, 4, 128])
    conv_w = dt_in("conv_w", [DEPTH, CONV_W, D_CONV])
    conv_b = dt_in("conv_b", [DEPTH, D_CONV])
    conv_ln_g = dt_in("conv_ln_g", [DEPTH, D_CONV])
    conv_ln_b = dt_in("conv_ln_b", [DEPTH, D_CONV])
    conv_pw_w = dt_in("conv_pw_w", [DEPTH, D_CONV, D_CONV])
    conv_pw_b = dt_in("conv_pw_b", [DEPTH, D_CONV])
    w_out = dt_in("w_out", [DEPTH, D_MODEL, D_MODEL])
    b_out = dt_in("b_out", [DEPTH, D_MODEL])
    ln_g = dt_in("ln_g", [DEPTH, D_MODEL])
    ln_b = dt_in("ln_b", [DEPTH, D_MODEL])
    c_ident = dt_in("c_ident", [128, 128])
    c_triu = dt_in("c_triu", [128, 128])
    c_invw = dt_in("c_invw", [128, 2])
    c_invc = dt_in("c_invc", [128, 2, 15])

    yp = dt_out("yp", [nseq, S_len, D_MODEL])
    ys = dt_out("ys", [16, D_MODEL])
    npp = dt_out("npp", [DEPTH, nseq, 15, D_POOL])
    ncp = dt_out("ncp", [DEPTH, nseq, 30, D_CONV])
    nps = dt_out("nps", [DEPTH, 15, D_POOL])
    ncs = dt_out("ncs", [DEPTH, 30, D_CONV])
    nvs = dt_out("nvs", [DEPTH, 16, D_GMLP])

    scr_wout = nc.dram_tensor("scr_wout", [DEPTH, 128, 9 * 1024], BF16, kind="Internal").ap()

    NS = T // 128
    AW = 16 + T
    CW = 30 + T

    with contextlib.ExitStack() as st:
        def sb(name, shape, dt=F32):
            return st.enter_context(nc.sbuf_tensor(name, list(shape), dt))

        def psm(name):
            return st.enter_context(nc.psum_tensor(name, [128, 512], F32))

        x_tok = [sb("x_tok%d" % i, [128, NS, 1024]) for i in range(2)]
        xT = sb("xT", [128, 8, T], BF16)
        mixed = sb("mixed", [128, 9, T], BF16)
        wring = [sb("wring%d" % i, [128, 8 * 384], BF16) for i in range(NRING)]
        wout_sb = sb("wout_sb", [128, 9, 1024], BF16)
        pw_sb = sb("pw_sb", [128, DEPTH, 3, 384], BF16)
        bd_sb = sb("bd_sb", [128, DEPTH, 2, 128], BF16)
        wmt_sb = sb("wmt_sb", [128, DEPTH, 4, 128], BF16)
        D_sb = sb("D_sb", [128, 3, CONV_W, 128], BF16)
        a_ext = [sb("a_ext%d" % l, [128, 2, AW]) for l in range(DEPTH)]
        c_ext = [sb("c_ext%d" % l, [128, 3, CW], BF16) for l in range(DEPTH)]
        tmpA = sb("tmpA", [128, AW])
        tmpB = sb("tmpB", [128, AW])
        dT = sb("dT", [128, 2, T], BF16)
        sga = sb("sga", [128, 2, T])
        vnf = [sb("vnf%d" % i, [128, 384]) for i in range(2)]
        vn = sb("vn", [128, NS, 384], BF16)
        tmpR = [sb("tmpR%d" % i, [128, T]) for i in range(3)]
        ug = [sb("ug%d" % i, [128, T]) for i in range(2)]
        sgb4 = sb("sgb4", [128, T])
        sgc = sb("sgc", [128, 3, T])
        yv = sb("yv", [128, 3, T])
        ysq = sb("ysq", [128, 3, T])
        mean_sb = sb("mean_sb", [128, T])
        rstd_sb = sb("rstd_sb", [128, T])
        ysil = sb("ysil", [128, 3, T], BF16)
        cf = sb("cf", [128, 3, 30])
        lnbc = sb("lnbc", [128, 2, 1024])
        glnbc = sb("glnbc", [128, 2, 384])
        ident = sb("ident", [128, 128])
        epsc = sb("epsc", [128, 1])
        lgb = sb("lgb", [128, DEPTH, 8, 2])
        triu = sb("triu", [128, 128])
        invw = sb("invw", [128, 2])
        invc = sb("invc", [128, 2, 15])
        onesf = sb("onesf", [128, 128])
        onesb = sb("onesb", [128, 128], BF16)
        cw_sb = sb("cw_sb", [128, DEPTH, 3, CONV_W])
        pscale = sb("pscale", [128, DEPTH, 2])
        cvec = sb("cvec", [128, DEPTH, 4, 3])
        hl = sb("hl", [2, 3072], BF16)
        stats = sb("stats", [128, 8, 32])
        vstats = sb("vstats", [128, 4, 16])
        sto = [sb("sto%d" % i, [32, 384]) for i in range(1)]

        banks = [psm("ps%d" % i) for i in range(8)]
        pwst = tmpA[:, 0:256].rearrange("p (l j c) -> p l j c", l=DEPTH, j=2)
        gst = tmpB[:, 0:128]
        sti = tmpR[0][0:32, 0:384]
        S = Sched()
        bank_ctr = [0]

        def nb():
            i = bank_ctr[0] % 8
            bank_ctr[0] += 1
            return banks[i], 'ps%d' % i

        stat_ctr = [0]

        def nstat():
            i = stat_ctr[0] % 8
            stat_ctr[0] += 1
            return stats[:, i:i + 1, :], 'stat%d' % i

        tr_ctr = [0]

        def ntmp():
            i = tr_ctr[0] % 3
            tr_ctr[0] += 1
            return tmpR[i], 'tmpR%d' % i

        sto_ctr = [0]

        S.op('sp', lambda e: e.dma_start(out=ident[:], in_=c_ident[:, :]), writes=['ident'], chan='k_id')
        S.op('sp', lambda e: e.dma_start(out=triu[:], in_=c_triu[:, :]), writes=['triu'], chan='k_tr')
        S.op('sp', lambda e: e.dma_start(out=invw[:], in_=c_invw[:, :]), writes=['invw'], chan='k_iw')
        S.op('sp', lambda e: e.dma_start(out=invc[:], in_=c_invc[:, :, :]), writes=['invc'], chan='k_ic')
        S.op('pool', lambda e: e.memset(onesf[:], 1.0 / 384.0), writes=['onesf'])
        S.op('pool', lambda e: e.memset(onesb[:], 1.0), writes=['onesb'])
        S.op('pool', lambda e: e.memset(epsc[:], LN_EPS), writes=['epsc'])
        S.op('pool', lambda e: e.memset(bd_sb[:], 0.0), writes=['bd'])

        def side_vec():
            def tr_rows(src_tile, nrows, col0, dst_ap_fn, rres, wres, part):
                pb, pbn = nb()
                S.op('pe', (lambda pb: lambda e: e.transpose(out=pb[:, 0:nrows], in_=src_tile[0:nrows, col0:col0 + 128], identity=ident[0:nrows, 0:nrows]))(pb),
                     reads=[rres, 'ident'], writes=[pbn])
                S.op('dve', (lambda pb: lambda e: e.tensor_copy(out=dst_ap_fn(), in_=pb[:, 0:nrows]))(pb), reads=[pbn], writes=[wres], partial=part)

            for l in range(DEPTH):
                st_t, st_n = tmpR[1 + l], 'tmpR%d' % (1 + l)
                S.op('sp', (lambda l, st_t: lambda e: e.dma_start(out=st_t[0:CONV_W, 0:384], in_=conv_w[l, :, :]))(l, st_t), writes=[st_n], chan='k_cw%d' % l)
                for cc in range(3):
                    tr_rows(st_t, CONV_W, cc * 128, (lambda l, cc: lambda: cw_sb[:, l, cc, :])(l, cc), st_n, 'cw', True)
            for vi, vec in enumerate((conv_b, conv_ln_g, conv_ln_b, conv_pw_b)):
                S.op('sp', (lambda vi, vec: lambda e: e.dma_start(out=ug[0][2 * vi:2 * vi + 2, 0:384], in_=vec[:, :]))(vi, vec), writes=['ug0'], chan='k_cv', partial=True)
            for cc in range(3):
                tr_rows(ug[0], 8, cc * 128, (lambda cc: lambda: cvec[:, :, :, cc].rearrange("p l v -> p v l"))(cc), 'ug0', 'cvec', True)
            S.op('sp', lambda e: e.dma_start(out=ug[1][0:2, 0:256], in_=pool_scale[:, :]), writes=['ug1'], chan='k_ps')
            for jc in range(2):
                tr_rows(ug[1], 2, jc * 128, (lambda jc: lambda: pscale[:, :, jc])(jc), 'ug1', 'pscale', True)
            sgaf = sga[:].rearrange("p a b -> p (a b)")
            for vi, vec in enumerate((ln_g, ln_b)):
                S.op('sp', (lambda vi, vec: lambda e: e.dma_start(out=sgaf[2 * vi:2 * vi + 2, 0:1024], in_=vec[:, :]))(vi, vec), writes=['sga'], chan='k_lg', partial=True)
            for kc in range(8):
                tr_rows(sgaf, 4, kc * 128, (lambda kc: lambda: lgb[:, :, kc, :].rearrange("p l v -> p v l"))(kc), 'sga', 'lgb', True)

        def side_pw():
            for l in range(DEPTH):
                for g in range(4):
                    j, r = g // 2, g % 2
                    S.op('sp', (lambda l, g, j, r: lambda e: e.dma_start(out=pwst[64 * r:64 * r + 64, l, j, :], in_=pool_w[l, g, :, :]))(l, g, j, r),
                         writes=['tmpA'], chan='k_pw', partial=True)
            for l in range(DEPTH):
                for g in range(4):
                    j, r = g // 2, g % 2
                    S.op('dve', (lambda l, j, r: lambda e: e.tensor_copy(out=bd_sb[64 * r:64 * r + 64, l, j, 64 * r:64 * r + 64],
                                                                         in_=pwst[64 * r:64 * r + 64, l, j, :]))(l, j, r),
                         reads=['tmpA'], writes=['bd'], partial=True)

        yvf = yv[:].rearrange("p a b -> p (a b)")
        ysqf = ysq[:].rearrange("p a b -> p (a b)")
        ysilf = ysil[:].rearrange("p a b -> p (a b)")
        YR = ['yv0', 'yv1', 'yv2']
        QR = ['ysq0', 'ysq1', 'ysq2']
        LR = ['ysil0', 'ysil1', 'ysil2']
        hl_jobs = []
        for l in range(DEPTH):
            hl_jobs.append((b_out[l, :].unsqueeze(0), 1024, l * 1024))
            hl_jobs.append((gmlp_b[l, :, :].rearrange("h t -> (h t)").unsqueeze(0), 512, 2048 + l * 512))
        def hl_job(jj):
            src, nel, c0 = hl_jobs[jj]
            S.op('sp', (lambda src, nel: lambda e: e.dma_start(out=yvf[0:1, 0:nel], in_=src))(src, nel), writes=YR, chan='k_hl')
            S.op('dve', (lambda nel, c0: lambda e: e.tensor_copy(out=hl[0:1, c0:c0 + nel], in_=yvf[0:1, 0:nel]))(nel, c0),
                 reads=YR, writes=['hl'], hard=True, partial=True)
            S.op('dve', (lambda nel, c0: lambda e: e.tensor_copy(out=ysqf[0:1, 0:nel], in_=hl[0:1, c0:c0 + nel]))(nel, c0), reads=['hl'], writes=QR, hard=True)
            S.op('dve', (lambda nel: lambda e: e.tensor_tensor(out=ysqf[0:1, 0:nel], in0=yvf[0:1, 0:nel], in1=ysqf[0:1, 0:nel], op=ALU.subtract))(nel),
                 reads=YR + QR, writes=QR, hard=True)
            S.op('dve', (lambda nel: lambda e: e.tensor_copy(out=ysilf[0:1, 0:nel], in_=ysqf[0:1, 0:nel]))(nel), reads=QR, writes=LR, hard=True)
            S.op('sp', (lambda nel, c0: lambda e: e.dma_start(out=hl[1:2, c0:c0 + nel], in_=ysilf[0:1, 0:nel]))(nel, c0), reads=LR, writes=['hl'], chan='k_hl2', partial=True)


        def gm_job(jj):
            l, h = jj // 4, jj % 4
            stg_t, stg_n = (yv, 'yv%d' % (jj % 3)) if False else (None, None)
            gsrc = sgc[:, jj // 4, (jj % 4) * 128:(jj % 4) * 128 + 128]
            gres = 'gm_stage%d' % jj
            pb, pbn = nb()
            S.op('sp', (lambda l, h, gsrc: lambda e: e.dma_start(out=gsrc, in_=gmlp_w[l, h, :, :]))(l, h, gsrc), writes=[gres, 'sgc'], chan='k_gst%d' % jj, partial=True)
            S.op('pe', (lambda pb, gsrc: lambda e: e.transpose(out=pb[:, 0:128], in_=gsrc, identity=ident[:]))(pb, gsrc),
                 reads=[gres, 'ident'], writes=[pbn])
            S.op('dve', (lambda l, h, pb: lambda e: e.tensor_tensor(out=wmt_sb[:, l, h, :], in0=pb[:, 0:128], in1=triu[:], op=ALU.mult))(l, h, pb),
                 reads=[pbn, 'triu'], writes=['wmt'], partial=True)


        ALLSCR = ['scr_wout%d' % l_ for l_ in range(DEPTH)]
        tiles = []
        for sq in range(nseq):
            for j in range(ntile):
                tiles.append(dict(kind='p', seq=sq, j=j, T=T, ns=NS, psz=128, first=(j == 0), last=(j == ntile - 1)))
        tiles.append(dict(kind='s', seq=0, j=0, T=16, ns=1, psz=16, first=False, last=True))
        ntl = len(tiles) * DEPTH
        gseq = [(n, g) for n in range(ntl) for g in range(NG)]

        def emit_group_load(q):
            if q >= len(gseq):
                return
            n, g = gseq[q]
            l = n % DEPTH
            gn, gc0, gw = GROUPS[g]
            slot = q % NRING
            S.op('pool', (lambda l, slot, gc0, gw: lambda e: e.dma_start(out=wring[slot][:, 0:8 * gw].rearrange("p (k e) -> p k e", k=8),
                                                                       in_=w_in[l, :, gc0:gc0 + gw].rearrange("(k p) e -> p k e", p=128)))(l, slot, gc0, gw),
                 writes=['wring%d' % slot], chan='ring%d' % slot)

        def emit_x_load(ti):
            if ti >= len(tiles):
                return
            tl = tiles[ti]
            slot = ti % 2
            if tl['kind'] == 'p':
                for s in range(NS):
                    r0 = tl['j'] * T + s * 128
                    S.op('sp', (lambda slot, s, sq, r0: lambda e: e.dma_start(out=x_tok[slot][:, s, :], in_=xp[sq, r0:r0 + 128, :]))(slot, s, tl['seq'], r0),
                         writes=['x_tok%d_%d' % (slot, s)], chan='x%d_%d' % (slot, s))
            else:
                S.op('sp', (lambda slot: lambda e: e.dma_start(out=x_tok[slot][0:16, 0, :], in_=xs[:, :]))(slot),
                     writes=['x_tok%d_0' % slot], chan='x%d_0' % slot)

        def emit_wout_chunk(n, k):
            l = n % DEPTH
            S.op('sp', (lambda l, k: lambda e: e.dma_start(out=wout_sb[:, k, :], in_=scr_wout[l, :, k * 1024:(k + 1) * 1024]))(l, k),
                 reads=ALLSCR, writes=['wout_sb'], chan='wout', partial=(k != 0))

        def emit_lnbc_load(n):
            if n >= ntl:
                return
            l = n % DEPTH
            S.op('sp', (lambda l: lambda e: e.dma_start(out=lnbc[:, 0, :], in_=ln_g[l, :].partition_broadcast(128)))(l), writes=['lnbc'], chan='lnbc')
            S.op('sp', (lambda l: lambda e: e.dma_start(out=lnbc[:, 1, :], in_=ln_b[l, :].partition_broadcast(128)))(l), writes=['lnbc'], chan='lnbc', partial=True)

        def emit_glnbc_load(n):
            if n >= ntl:
                return
            l = n % DEPTH
            S.op('sp', (lambda l: lambda e: e.dma_start(out=glnbc[:, 0, :], in_=gmlp_ln_g[l, :].partition_broadcast(128)))(l), writes=['glnbc'], chan='glnbc')
            S.op('sp', (lambda l: lambda e: e.dma_start(out=glnbc[:, 1, :], in_=gmlp_ln_b[l, :].partition_broadcast(128)))(l), writes=['glnbc'], chan='glnbc', partial=True)

        def emit_D_regen(n, chunks=(0, 1, 2)):
            if n >= ntl:
                return
            l = n % DEPTH
            for cc in chunks:
                S.op('pool', (lambda l, cc: lambda e: e.tensor_tensor(out=D_sb[:, cc, :, :], in0=ident[:].unsqueeze(1).to_broadcast([128, CONV_W, 128]),
                                                                      in1=cw_sb[:, l, cc, :].unsqueeze(2).to_broadcast([128, CONV_W, 128]), op=ALU.mult))(l, cc),
                     reads=['ident', 'cw'], writes=['D%d' % cc])

        def rsqrt_dve(x, y, t, rx, ry, rt, hard):
            xi, yi = x.bitcast(I32), y.bitcast(I32)
            S.op('dve', lambda e: e.tensor_scalar(out=yi, in0=xi, scalar1=1, scalar2=None, op0=ALU.arith_shift_right), reads=rx, writes=ry, hard=hard)
            S.op('dve', lambda e: e.tensor_scalar(out=yi, in0=yi, scalar1=-1, scalar2=0x5f3759df, op0=ALU.mult, op1=ALU.add), reads=ry, writes=ry, hard=hard)
            for _ in range(3):
                S.op('dve', lambda e: e.tensor_tensor(out=t, in0=y, in1=y, op=ALU.mult), reads=ry, writes=rt, hard=hard)
                S.op('dve', lambda e: e.tensor_tensor(out=t, in0=t, in1=x, op=ALU.mult), reads=rt + rx, writes=rt, hard=hard)
                S.op('dve', lambda e: e.tensor_scalar(out=t, in0=t, scalar1=-0.5, scalar2=1.5, op0=ALU.mult, op1=ALU.add), reads=rt, writes=rt, hard=hard)
                S.op('dve', lambda e: e.tensor_tensor(out=y, in0=y, in1=t, op=ALU.mult), reads=ry + rt, writes=ry, hard=hard)

        def ln_small(psz, mv, nm, ncol=1):
            S.op('act', lambda e: e.activation(out=mv[0:psz, :, 3:4], in_=mv[0:psz, :, 1:2], func=AF.Sqrt, bias=epsc[0:psz, 0:1], scale=1.0),
                 reads=[nm, 'epsc'], writes=[nm], hard=True)
            S.op('dve', lambda e: e.reciprocal(out=mv[0:psz, :, 2:3], in_=mv[0:psz, :, 3:4]), reads=[nm], writes=[nm], hard=True)

        def state_out(src_fn, nch, ncol, dst_ap, reads):
            i = 0
            sto_ctr[0] += 1
            pb, pbn = nb()
            for ch in range(nch):
                S.op('pe', (lambda ch, pb: lambda e: e.transpose(out=pb[0:ncol, ch * 128:(ch + 1) * 128], in_=src_fn(ch), identity=ident[:]))(ch, pb),
                     reads=list(reads) + ['ident'], writes=[pbn])
            S.op('act', (lambda i, pb: lambda e: e.copy(out=sto[i][0:ncol, 0:nch * 128], in_=pb[0:ncol, 0:nch * 128]))(i, pb),
                 reads=[pbn], writes=['sto%d' % i])
            S.op('sp', (lambda i: lambda e: e.dma_start(out=dst_ap, in_=sto[i][0:ncol, 0:nch * 128]))(i),
                 reads=['sto%d' % i], chan='sto%d' % i)

        def emit_transposes(ti, s, affine_l=None):
            tl = tiles[ti]
            psz = tl['psz']
            slot = ti % 2
            xt = x_tok[slot]
            xr = 'x_tok%d_%d' % (slot, s)
            for hb in range(2):
                pb, pbn = nb()
                for q in range(4):
                    kc = hb * 4 + q
                    S.op('pe', (lambda kc, q, pb: lambda e: e.transpose(out=pb[:, q * 128:q * 128 + psz], in_=xt[0:psz, s, kc * 128:(kc + 1) * 128],
                                                                        identity=ident[0:psz, 0:psz]))(kc, q, pb),
                         reads=[xr, 'ident'], writes=[pbn])
                if affine_l is None:
                    S.op('act', (lambda hb, pb: lambda e: e.copy(out=xT[:, hb * 4:hb * 4 + 4, s * 128:s * 128 + psz],
                                                                 in_=pb[:].rearrange("p (q t) -> p q t", q=4)[:, :, 0:psz]))(hb, pb),
                         reads=[pbn], writes=['xT_%d' % s])
                else:
                    for q in range(4):
                        kc = hb * 4 + q
                        S.op('act', (lambda kc, q, pb: lambda e: e.activation(out=xT[:, kc, s * 128:s * 128 + psz], in_=pb[:, q * 128:q * 128 + psz], func=AF.Identity,
                                                                              bias=lgb[:, affine_l, kc, 1:2], scale=lgb[:, affine_l, kc, 0:1]))(kc, q, pb),
                             reads=[pbn, 'lgb'], writes=['xT_%d' % s], partial=(q != 0 or hb != 0))

        emit_x_load(0)
        for s_ in range(tiles[0]['ns']):
            emit_transposes(0, s_)
        emit_glnbc_load(0)
        side_vec()
        for jj in range(8):
            gm_job(jj)
        side_pw()
        for jj in range(len(hl_jobs)):
            hl_job(jj)
        WO_ROWS = [(0, 128), (128, 128), (256, 96), (352, 96), (448, 96), (544, 96), (640, 128), (768, 128), (896, 128)]
        for l in range(DEPTH):
            for ch, (r0, rn) in enumerate(WO_ROWS):
                S.op('pool', (lambda l, ch, r0, rn: lambda e: e.dma_start(out=scr_wout[l, :, ch * 1024:(ch + 1) * 1024], in_=w_out[l, r0:r0 + 128, :]))(l, ch, r0, rn),
                     writes=['scr_wout%d' % l], chan='k_wo%d' % l, partial=True)
        for l in range(DEPTH):
            S.op('pool', (lambda l: lambda e: e.dma_start(out=pw_sb[:, l, :, :], in_=conv_pw_w[l, :, :].rearrange("(k p) e -> p k e", p=128)))(l),
                 writes=['pw_sb'], chan='k_pwc', partial=True)
        for l in range(DEPTH):
            S.op('sp', (lambda l: lambda e: e.dma_start(out=scr_wout[l, 96:98, 2048:3072], in_=hl[0:2, l * 1024:(l + 1) * 1024]))(l),
                 reads=['hl'], writes=['scr_wout%d' % l], chan='k_bo2_%d' % l)
        for q in range(T // 128):
            S.op('sp', (lambda q: lambda e: e.dma_start(out=mixed[96:98, 2, q * 128:(q + 1) * 128], in_=onesb[0:2, 0:128]))(q),
                 reads=['onesb'], writes=['mixed_b'], chan='k_ones', partial=True)
        qload = [0]
        for _ in range(NRING):
            emit_group_load(qload[0])
            qload[0] += 1
        emit_D_regen(0)

        def group_ready(q):
            return wring[q % NRING], 'wring%d' % (q % NRING)

        def tile_layer(ti, tl, l, n):
            Tt, ns, psz = tl['T'], tl['ns'], tl['psz']
            slot = ti % 2
            xt = x_tok[slot]
            xres = ['x_tok%d_%d' % (slot, s) for s in range(ns)]
            ae, ce = a_ext[l], c_ext[l]
            aen, cen = 'a_ext%d' % l, 'c_ext%d' % l
            if tl['kind'] == 'p' and tl['first']:
                S.op('pool', (lambda ae: lambda e: e.memset(ae[:, :, 0:16], 0.0))(ae), writes=[aen])
                S.op('pool', (lambda ce: lambda e: e.memset(ce[:, :, 0:30], 0.0))(ce), writes=[cen])
            elif tl['kind'] == 's':
                S.op('sp', (lambda l: lambda e: e.dma_start(out=sti[0:15, 0:256], in_=st_pool[l, :, :]))(l), writes=['tmpR0'], chan='sti')
                pb, pbn = nb()
                for jc in range(2):
                    S.op('pe', (lambda jc, pb: lambda e: e.transpose(out=pb[:, jc * 16:jc * 16 + 15], in_=sti[0:15, jc * 128:(jc + 1) * 128], identity=ident[0:15, 0:15]))(jc, pb),
                         reads=['tmpR0', 'ident'], writes=[pbn])
                S.op('dve', (lambda ae, pb: lambda e: e.tensor_copy(out=ae[:, :, 1:16], in_=pb[:, 0:32].rearrange("p (j c) -> p j c", j=2)[:, :, 0:15]))(ae, pb),
                     reads=[pbn], writes=[aen])
                S.op('sp', (lambda l: lambda e: e.dma_start(out=sti[0:30, 0:384], in_=st_conv[l, :, :]))(l), reads=[], writes=['tmpR0'], chan='sti')
                pb2, pbn2 = nb()
                for cc in range(3):
                    S.op('pe', (lambda cc, pb2: lambda e: e.transpose(out=pb2[:, cc * 32:cc * 32 + 30], in_=sti[0:30, cc * 128:(cc + 1) * 128], identity=ident[0:30, 0:30]))(cc, pb2),
                         reads=['tmpR0', 'ident'], writes=[pbn2])
                S.op('dve', (lambda ce, pb2: lambda e: e.tensor_copy(out=ce[:, :, 0:30], in_=pb2[:, 0:96].rearrange("p (j c) -> p j c", j=3)[:, :, 0:30]))(ce, pb2),
                     reads=[pbn2], writes=[cen])
                S.op('sp', (lambda l: lambda e: e.dma_start(out=ncs[l, 0:14, :], in_=st_conv[l, 16:30, :]))(l), chan='ncs_cp')
            else:
                S.op('pool', (lambda ae: lambda e: e.tensor_copy(out=ae[:, :, 1:16], in_=ae[:, :, T + 1:T + 16]))(ae), reads=[aen], writes=[aen])
                S.op('pool', (lambda ce: lambda e: e.tensor_copy(out=ce[:, :, 0:30], in_=ce[:, :, T:T + 30]))(ce), reads=[cen], writes=[cen])

            xTres = ['xT_%d' % s for s in range(ns)]

            qbase = n * NG

            def use_group(gidx):
                q = qbase + gidx
                return group_ready(q)

            def done_group(gidx):
                emit_group_load(qload[0])
                qload[0] += 1
                emit_wout_chunk(n, gidx)
                if gidx == 7:
                    emit_wout_chunk(n, 8)
                if gidx == 2 and l == DEPTH - 1:
                    emit_x_load(ti + 1)
                if gidx == 4:
                    emit_lnbc_load(n)
                if gidx == 5:
                    emit_glnbc_load(n + 1)

            wt, wres = use_group(0)
            wv = wt[:, 0:8 * 384].rearrange("p (k e) -> p k e", k=8)
            vps = []
            for s in range(ns):
                pb, pbn = nb()
                for kc in range(8):
                    S.op('pe', (lambda s, kc, pb, wv: lambda e: e.matmul(pb[0:psz, 0:384], lhsT=xT[:, kc, s * 128:s * 128 + psz], rhs=wv[:, kc, :],
                                                                         start=(kc == 0), stop=(kc == 7)))(s, kc, pb, wv),
                         reads=[xTres[s], wres], writes=[pbn])
                S.op('dve', (lambda s, pb: lambda e: e.bn_stats(out=vstats[0:psz, s, 8:14], in_=pb[0:psz, 0:384]))(s, pb), reads=[pbn], writes=['vstats'], hard=True, partial=(s != 0))
                S.op('dve', (lambda s: lambda e: e.bn_aggr(out=vstats[0:psz, s, 0:2], in_=vstats[0:psz, s, 8:14]))(s), reads=['vstats'], writes=['vstats'], hard=True, partial=True)
                vps.append((pb, pbn))
            ln_small(psz, vstats[:, 0:ns, :], 'vstats', ns)
            for s in range(ns):
                pb, pbn = vps[s]
                vf = vnf[s % 2]
                vfn = 'vnf%d' % (s % 2)
                S.op('dve', (lambda s, pb, vf: lambda e: e.tensor_scalar(out=vf[0:psz, :], in0=pb[0:psz, 0:384], scalar1=vstats[0:psz, s, 0:1], scalar2=vstats[0:psz, s, 2:3],
                                                                         op0=ALU.subtract, op1=ALU.mult))(s, pb, vf),
                     reads=[pbn, 'vstats'], writes=[vfn])
                S.op('dve', (lambda vf: lambda e: e.tensor_tensor(out=vf[0:psz, :], in0=vf[0:psz, :], in1=glnbc[0:psz, 0, :], op=ALU.mult))(vf),
                     reads=[vfn, 'glnbc'], writes=[vfn])
                if tl['kind'] == 's':
                    S.op('dve', (lambda vf: lambda e: e.tensor_tensor(out=vf[0:psz, :], in0=vf[0:psz, :], in1=glnbc[0:psz, 1, :], op=ALU.add))(vf),
                         reads=[vfn, 'glnbc'], writes=[vfn])
                    S.op('dve', (lambda s, vf: lambda e: e.tensor_copy(out=vn[0:psz, s, :], in_=vf[0:psz, :]))(s, vf), reads=[vfn], writes=['vn_%d' % s])
                    S.op('sp', (lambda l, vf: lambda e: e.dma_start(out=nvs[l, :, :], in_=vf[0:16, :]))(l, vf), reads=[vfn], chan='nvs')
                else:
                    S.op('dve', (lambda s, vf: lambda e: e.tensor_tensor(out=vn[0:psz, s, :], in0=vf[0:psz, :], in1=glnbc[0:psz, 1, :], op=ALU.add))(s, vf),
                         reads=[vfn, 'glnbc'], writes=['vn_%d' % s])
            done_group(0)

            wt2, wres2 = use_group(1)
            wc2 = wt2[:, 0:8 * 384].rearrange("p (k e) -> p k e", k=8)
            sigs = []
            for cc in range(3):
                pb, pbn = nb()
                for kc in range(8):
                    S.op('pe', (lambda cc, kc, pb, wc2: lambda e: e.matmul(pb[:, 0:Tt], lhsT=wc2[:, kc, cc * 128:(cc + 1) * 128], rhs=xT[:, kc, 0:Tt],
                                                                           start=(kc == 0), stop=(kc == 7)))(cc, kc, pb, wc2),
                         reads=xTres + [wres2], writes=[pbn])
                tt, ttn = ntmp()
                S.op('act', (lambda pb, tt: lambda e: e.activation(out=tt[:, 0:Tt], in_=pb[:, 0:Tt], func=AF.Sigmoid))(pb, tt), reads=[pbn], writes=[ttn])
                sigs.append((tt, ttn))
            done_group(1)
            wt3, wres3 = use_group(2)
            wc1 = wt3[:, 0:8 * 384].rearrange("p (k e) -> p k e", k=8)
            ntail = min(30, Tt)
            for cc in range(3):
                pb, pbn = nb()
                for kc in range(8):
                    S.op('pe', (lambda cc, kc, pb, wc1: lambda e: e.matmul(pb[:, 0:Tt], lhsT=wc1[:, kc, cc * 128:(cc + 1) * 128], rhs=xT[:, kc, 0:Tt],
                                                                           start=(kc == 0), stop=(kc == 7)))(cc, kc, pb, wc1),
                         reads=xTres + [wres3], writes=[pbn])
                tt, ttn = sigs[cc]
                S.op('dve', (lambda cc, pb, tt, ce: lambda e: e.tensor_tensor(out=ce[:, cc, 30:30 + Tt], in0=pb[:, 0:Tt], in1=tt[:, 0:Tt], op=ALU.mult))(cc, pb, tt, ce),
                     reads=[pbn, ttn], writes=[cen], partial=True)
                if tl['last']:
                    S.op('dve', (lambda cc, pb, tt: lambda e: e.tensor_tensor(out=cf[:, cc, 0:ntail], in0=pb[:, Tt - ntail:Tt], in1=tt[:, Tt - ntail:Tt], op=ALU.mult))(cc, pb, tt),
                         reads=[pbn, ttn], writes=['cf'], partial=(cc != 0))
            done_group(2)
            if tl['last']:
                if tl['kind'] == 'p':
                    state_out(lambda ch: cf[:, ch, 0:30], 3, 30, ncp[l, tl['seq'], :, :], ['cf'])
                else:
                    state_out(lambda ch: cf[:, ch, 0:16], 3, 16, ncs[l, 14:30, :], ['cf'])

            wt4, wres4 = use_group(3)
            wa = wt4[:, 0:8 * 256].rearrange("p (k e) -> p k e", k=8)
            for jc in range(2):
                pb, pbn = nb()
                for kc in range(8):
                    S.op('pe', (lambda jc, kc, pb, wa: lambda e: e.matmul(pb[:, 0:Tt], lhsT=wa[:, kc, jc * 128:(jc + 1) * 128], rhs=xT[:, kc, 0:Tt],
                                                                          start=(kc == 0), stop=(kc == 7)))(jc, kc, pb, wa),
                         reads=xTres + [wres4], writes=[pbn])
                S.op('act', (lambda jc, pb, ae: lambda e: e.copy(out=ae[:, jc, 16:16 + Tt], in_=pb[:, 0:Tt]))(jc, pb, ae), reads=[pbn], writes=[aen], partial=True)
            done_group(3)
            W = 16 + Tt
            for jc in range(2):
                a_ = ae[:, jc, :]
                nlev = 2 if jc == 0 else 4
                src = a_
                bufs = [tmpA, tmpB]
                bi = 0
                for lev in range(1, nlev + 1):
                    sh = 1 << (lev - 1)
                    lo = (1 << lev)
                    dst = bufs[bi]
                    full = (lev < nlev)
                    p0 = 0 if full else 64
                    S.op('pool', (lambda src, dst, sh, lo, p0, W: lambda e: e.tensor_tensor(out=dst[p0:128, lo:W], in0=src[p0:128, lo:W], in1=src[p0:128, lo - sh:W - sh], op=ALU.add))(src, dst, sh, lo, p0, W),
                         reads=[aen, 'tmpA', 'tmpB'], writes=['tmpA' if bi == 0 else 'tmpB'])
                    if lev == nlev - 1:
                        low_src = dst
                    src = dst
                    bi ^= 1
                hi_src = src
                for (p0, p1, sr) in ((0, 64, low_src), (64, 128, hi_src)):
                    S.op('dve', (lambda jc, p0, p1, sr, a_: lambda e: e.scalar_tensor_tensor(out=dT[p0:p1, jc, 0:Tt], in0=sr[p0:p1, 16:16 + Tt], scalar=invw[p0:p1, jc:jc + 1],
                                                                                          in1=a_[p0:p1, 16:16 + Tt], op0=ALU.mult, op1=ALU.subtract))(jc, p0, p1, sr, a_),
                         reads=[aen, 'tmpA', 'tmpB', 'invw'], writes=['dT'], partial=not (jc == 0 and p0 == 0))
                    if tl['kind'] == 'p' and tl['first']:
                        S.op('pool', (lambda jc, p0, p1, sr: lambda e: e.tensor_tensor(out=cf[p0:p1, 0, 0:15], in0=sr[p0:p1, 16:31], in1=invc[p0:p1, jc, :], op=ALU.mult))(jc, p0, p1, sr),
                             reads=['tmpA', 'tmpB', 'invc', 'cf'], writes=['cf'], hard=True)
                        S.op('pool', (lambda jc, p0, p1, a_: lambda e: e.tensor_tensor(out=dT[p0:p1, jc, 0:15], in0=cf[p0:p1, 0, 0:15], in1=a_[p0:p1, 16:31], op=ALU.subtract))(jc, p0, p1, a_),
                             reads=['cf', aen, 'dT'], writes=['dT'], partial=True)
            if tl['last']:
                if tl['kind'] == 'p':
                    state_out(lambda ch: ae[:, ch, Tt + 1:Tt + 16], 2, 15, npp[l, tl['seq'], :, :], [aen])
                else:
                    state_out(lambda ch: ae[:, ch, Tt + 1:Tt + 16], 2, 15, nps[l, :, :], [aen])
            for cc in range(3):
                pb, pbn = nb()
                for k in range(CONV_W):
                    S.op('pe', (lambda cc, k, pb, ce: lambda e: e.matmul(pb[:, 0:Tt], lhsT=D_sb[:, cc, k, :], rhs=ce[:, cc, k:k + Tt], start=(k == 0), stop=(k == CONV_W - 1)))(cc, k, pb, ce),
                         reads=[cen, 'D%d' % cc], writes=[pbn])
                S.op('act', (lambda cc, pb: lambda e: e.activation(out=yv[:, cc, 0:Tt], in_=pb[:, 0:Tt], func=AF.Identity, bias=cvec[:, l, 0, cc:cc + 1], scale=1.0))(cc, pb),
                     reads=[pbn, 'cvec'], writes=['yv%d' % cc])
                S.op('act', (lambda cc, pb: lambda e: e.activation(out=ysq[:, cc, 0:Tt], in_=pb[:, 0:Tt], func=AF.Square, bias=cvec[:, l, 0, cc:cc + 1], scale=1.0))(cc, pb),
                     reads=[pbn, 'cvec'], writes=['ysq%d' % cc])
                emit_D_regen(n + 1, (cc,))
            pm, pmn = nb()
            for cc in range(3):
                S.op('pe', (lambda cc, pm: lambda e: e.matmul(pm[:, 0:Tt], lhsT=onesf[:], rhs=yv[:, cc, 0:Tt], start=(cc == 0), stop=(cc == 2)))(cc, pm),
                     reads=['yv%d' % cc, 'onesf'], writes=[pmn])
            pq, pqn = nb()
            for cc in range(3):
                S.op('pe', (lambda cc, pq: lambda e: e.matmul(pq[:, 0:Tt], lhsT=onesf[:], rhs=ysq[:, cc, 0:Tt], start=(cc == 0), stop=(cc == 2)))(cc, pq),
                     reads=['ysq%d' % cc, 'onesf'], writes=[pqn])
            S.op('act', (lambda pm: lambda e: e.copy(out=mean_sb[:, 0:Tt], in_=pm[:, 0:Tt]))(pm), reads=[pmn], writes=['mean_sb'])
            for cc in range(3):
                S.op('pool', (lambda cc: lambda e: e.tensor_tensor(out=yv[:, cc, 0:Tt], in0=yv[:, cc, 0:Tt], in1=mean_sb[:, 0:Tt], op=ALU.subtract))(cc),
                     reads=['yv%d' % cc, 'mean_sb'], writes=['yv%d' % cc])
            cvt = ysq[:, 0, 0:Tt]
            cvt2 = ysq[:, 1, 0:Tt]
            S.op('dve', (lambda cvt: lambda e: e.tensor_tensor(out=cvt, in0=mean_sb[:, 0:Tt], in1=mean_sb[:, 0:Tt], op=ALU.mult))(cvt), reads=['mean_sb'], writes=['ysq0'])
            S.op('dve', (lambda pq, cvt: lambda e: e.tensor_tensor(out=cvt, in0=pq[:, 0:Tt], in1=cvt, op=ALU.subtract))(pq, cvt), reads=[pqn, 'ysq0'], writes=['ysq0'])
            S.op('act', (lambda cvt, cvt2: lambda e: e.activation(out=cvt2, in_=cvt, func=AF.Sqrt, bias=epsc[:, 0:1], scale=1.0))(cvt, cvt2), reads=['ysq0', 'epsc'], writes=['ysq1'])
            S.op('dve', (lambda cvt2: lambda e: e.reciprocal(out=rstd_sb[:, 0:Tt], in_=cvt2))(cvt2), reads=['ysq1'], writes=['rstd_sb'])
            wt4b, wres4b = use_group(4)
            wga = wt4b[:, 0:8 * 256].rearrange("p (k e) -> p k e", k=8)
            for jc in range(2):
                pb, pbn = nb()
                for kc in range(8):
                    S.op('pe', (lambda jc, kc, pb, wga: lambda e: e.matmul(pb[:, 0:Tt], lhsT=wga[:, kc, jc * 128:(jc + 1) * 128], rhs=xT[:, kc, 0:Tt],
                                                                           start=(kc == 0), stop=(kc == 7)))(jc, kc, pb, wga),
                         reads=xTres + [wres4b], writes=[pbn])
                S.op('act', (lambda jc, pb: lambda e: e.activation(out=sga[:, jc, 0:Tt], in_=pb[:, 0:Tt], func=AF.Silu))(jc, pb), reads=[pbn], writes=['sga'], partial=(jc != 0))
            done_group(4)
            wt5, wres5 = use_group(5)
            wgb = wt5[:, 0:8 * 384].rearrange("p (k e) -> p k e", k=8)
            sgbs = []
            for h in range(4):
                pb, pbn = nb()
                for kc in range(8):
                    S.op('pe', (lambda h, kc, pb, wgb: lambda e: e.matmul(pb[0:96, 0:Tt], lhsT=wgb[:, kc, h * 96:(h + 1) * 96], rhs=xT[:, kc, 0:Tt],
                                                                          start=(kc == 0), stop=(kc == 7)))(h, kc, pb, wgb),
                         reads=xTres + [wres5], writes=[pbn])
                if h < 3:
                    tt, ttn = ntmp()
                else:
                    tt, ttn = sgb4, 'sgb4'
                S.op('act', (lambda pb, tt: lambda e: e.activation(out=tt[0:96, 0:Tt], in_=pb[0:96, 0:Tt], func=AF.Silu))(pb, tt), reads=[pbn], writes=[ttn])
                sgbs.append((tt, ttn))
            done_group(5)
            wt6, wres6 = use_group(6)
            wu = wt6[:, 0:8 * 384].rearrange("p (k e) -> p k e", k=8)
            for h in range(4):
                pb, pbn = nb()
                for kc in range(8):
                    S.op('pe', (lambda h, kc, pb, wu: lambda e: e.matmul(pb[0:96, 0:Tt], lhsT=wu[:, kc, h * 96:(h + 1) * 96], rhs=xT[:, kc, 0:Tt],
                                                                         start=(kc == 0), stop=(kc == 7)))(h, kc, pb, wu),
                         reads=xTres + [wres6], writes=[pbn])
                tt, ttn = sgbs[h]
                ugt, ugn = ug[h % 2], 'ug%d' % (h % 2)
                S.op('dve', (lambda pb, tt, ugt: lambda e: e.tensor_tensor(out=ugt[0:96, 0:Tt], in0=pb[0:96, 0:Tt], in1=tt[0:96, 0:Tt], op=ALU.mult))(pb, tt, ugt),
                     reads=[pbn, ttn], writes=[ugn])
                pz, pzn = nb()
                o0 = 2048 + l * 512 + h * 128
                S.op('pe', (lambda pz, o0: lambda e: e.matmul(pz[0:96, 0:ns * 128].rearrange("p (s t) -> p s t", s=ns)[:, :, 0:psz], lhsT=onesb[0:2, 0:96],
                                                              rhs=hl[0:2, o0:o0 + psz].unsqueeze(1).to_broadcast([2, ns, psz]), start=True, stop=False))(pz, o0),
                     reads=['onesb', 'hl'], writes=[pzn])
                for s in range(ns):
                    S.op('pe', (lambda h, s, pz: lambda e: e.matmul(pz[0:96, s * 128:s * 128 + psz], lhsT=vn[0:psz, s, h * 96:(h + 1) * 96], rhs=wmt_sb[0:psz, l, h, 0:psz],
                                                                    start=False, stop=(s == ns - 1)))(h, s, pz),
                         reads=['vn_%d' % s, 'wmt'], writes=[pzn])
                S.op('dve', (lambda h, pz, ugt: lambda e: e.tensor_tensor(out=mixed[0:96, 2 + h, 0:Tt], in0=pz[0:96, 0:Tt], in1=ugt[0:96, 0:Tt], op=ALU.mult))(h, pz, ugt),
                     reads=[pzn, ugn], writes=['mixed_b'], partial=(h != 0))
                S.flush(3)
            done_group(6)

            wt7, wres7 = use_group(7)
            wgc = wt7[:, 0:8 * 384].rearrange("p (k e) -> p k e", k=8)
            for cc in range(3):
                pb, pbn = nb()
                for kc in range(8):
                    S.op('pe', (lambda cc, kc, pb, wgc: lambda e: e.matmul(pb[:, 0:Tt], lhsT=wgc[:, kc, cc * 128:(cc + 1) * 128], rhs=xT[:, kc, 0:Tt],
                                                                           start=(kc == 0), stop=(kc == 7)))(cc, kc, pb, wgc),
                         reads=xTres + [wres7], writes=[pbn])
                S.op('act', (lambda cc, pb: lambda e: e.activation(out=sgc[:, cc, 0:Tt], in_=pb[:, 0:Tt], func=AF.Silu))(cc, pb), reads=[pbn], writes=['sgc'], partial=(cc != 0))
                S.flush(2)
            done_group(7)

            S.flush()
            for cc in range(3):
                S.op('dve', (lambda cc: lambda e: e.tensor_tensor(out=yv[:, cc, 0:Tt], in0=yv[:, cc, 0:Tt], in1=rstd_sb[:, 0:Tt], op=ALU.mult))(cc),
                     reads=['yv%d' % cc, 'rstd_sb'], writes=['yv%d' % cc])
                S.op('act', (lambda cc: lambda e: e.activation(out=ysil[:, cc, 0:Tt], in_=yv[:, cc, 0:Tt], func=AF.Silu, bias=cvec[:, l, 2, cc:cc + 1], scale=cvec[:, l, 1, cc:cc + 1]))(cc),
                     reads=['yv%d' % cc, 'cvec'], writes=['ysil%d' % cc])
            for jc in range(2):
                pb, pbn = nb()
                S.op('pe', (lambda jc, pb: lambda e: e.matmul(pb[:, 0:Tt], lhsT=bd_sb[:, l, jc, :], rhs=dT[:, jc, 0:Tt], start=True, stop=True))(jc, pb),
                     reads=['dT', 'bd'], writes=[pbn])
                S.op('dve', (lambda jc, pb: lambda e: e.scalar_tensor_tensor(out=mixed[:, jc, 0:Tt], in0=pb[:, 0:Tt], scalar=pscale[:, l, jc:jc + 1], in1=sga[:, jc, 0:Tt],
                                                                             op0=ALU.mult, op1=ALU.mult))(jc, pb),
                     reads=[pbn, 'pscale', 'sga'], writes=['mixed_a'], partial=(jc != 0))

            for eo in range(3):
                pb, pbn = nb()
                for kc in range(3):
                    S.op('pe', (lambda eo, kc, pb: lambda e: e.matmul(pb[:, 0:Tt], lhsT=pw_sb[:, l, kc, eo * 128:(eo + 1) * 128], rhs=ysil[:, kc, 0:Tt], start=(kc == 0), stop=(kc == 2)))(eo, kc, pb),
                         reads=['ysil%d' % kc, 'pw_sb'], writes=[pbn])
                S.op('dve', (lambda eo, pb: lambda e: e.scalar_tensor_tensor(out=mixed[:, 6 + eo, 0:Tt], in0=pb[:, 0:Tt], scalar=cvec[:, l, 3, eo:eo + 1], in1=sgc[:, eo, 0:Tt],
                                                                             op0=ALU.add, op1=ALU.mult))(eo, pb),
                     reads=[pbn, 'cvec', 'sgc'], writes=['mixed_c'], partial=(eo != 0))
            pending = []

            def pool_affine(s):
                S.op('pool', (lambda s: lambda e: e.tensor_tensor(out=xt[0:psz, s, :], in0=xt[0:psz, s, :], in1=lnbc[0:psz, 0, :], op=ALU.mult))(s),
                     reads=[xres[s], 'lnbc'], writes=[xres[s]])
                S.op('pool', (lambda s: lambda e: e.tensor_tensor(out=xt[0:psz, s, :], in0=xt[0:psz, s, :], in1=lnbc[0:psz, 1, :], op=ALU.add))(s),
                     reads=[xres[s], 'lnbc'], writes=[xres[s]])

            KP = [128, 128, 98, 96, 96, 96, 128, 128, 128]
            mres = ['mixed_a', 'mixed_b', 'mixed_c']
            for s in range(ns):
                pos = []
                for hf in range(2):
                    pb, pbn = nb()
                    for kc in range(9):
                        kp = KP[kc]
                        S.op('pe', (lambda s, hf, kc, kp, pb: lambda e: e.matmul(pb[0:psz, 0:512], lhsT=mixed[0:kp, kc, s * 128:s * 128 + psz], rhs=wout_sb[0:kp, kc, hf * 512:(hf + 1) * 512],
                                                                                  start=(kc == 0), stop=(kc == 8)))(s, hf, kc, kp, pb),
                             reads=mres + ['wout_sb'], writes=[pbn])
                    pos.append((pb, pbn))
                stt, stn = nstat()
                for hf in range(2):
                    pb, pbn = pos[hf]
                    S.op('dve', (lambda s, hf, pb: lambda e: e.scalar_tensor_tensor(out=xt[0:psz, s, hf * 512:(hf + 1) * 512], in0=xt[0:psz, s, hf * 512:(hf + 1) * 512], scalar=ALPHA,
                                                                                    in1=pb[0:psz, 0:512], op0=ALU.mult, op1=ALU.add))(s, hf, pb),
                         reads=[pbn, xres[s]], writes=[xres[s]])
                    S.op('dve', (lambda s, hf, stt: lambda e: e.bn_stats(out=stt[0:psz, 0, 8 + 6 * hf:14 + 6 * hf], in_=xt[0:psz, s, hf * 512:(hf + 1) * 512]))(s, hf, stt),
                         reads=[xres[s]], writes=[stn], hard=True, partial=(hf == 1))
                S.op('dve', (lambda stt: lambda e: e.bn_aggr(out=stt[0:psz, 0, 0:2], in_=stt[0:psz, 0, 8:20]))(stt), reads=[stn], writes=[stn], hard=True)
                ln_small(psz, stt, stn)
                S.op('dve', (lambda stt: lambda e: e.tensor_scalar(out=stt[0:psz, 0, 5:6], in0=stt[0:psz, 0, 0:1], scalar1=stt[0:psz, 0, 2:3], scalar2=-1.0,
                                                                   op0=ALU.mult, op1=ALU.mult))(stt), reads=[stn], writes=[stn], hard=True)
                S.op('act', (lambda s, stt: lambda e: e.activation(out=xt[0:psz, s, :], in_=xt[0:psz, s, :], func=AF.Identity, bias=stt[0:psz, 0, 5:6], scale=stt[0:psz, 0, 2:3]))(s, stt),
                     reads=[xres[s], stn], writes=[xres[s]])
                if l == DEPTH - 1:
                    pool_affine(s)
                else:
                    pending.append(s)
                    if len(pending) > 1:
                        s0 = pending.pop(0)
                        emit_transposes(ti, s0, l)
                        pool_affine(s0)
                if l == DEPTH - 1:
                    if tl['kind'] == 'p':
                        r0 = tl['j'] * T + s * 128
                        S.op('pool', (lambda s, sq, r0: lambda e: e.dma_start(out=yp[sq, r0:r0 + 128, :], in_=xt[:, s, :]))(s, tl['seq'], r0),
                             reads=[xres[s]], chan='yo%d_%d' % (slot, s))
                    else:
                        S.op('sp', lambda e: e.dma_start(out=ys[:, :], in_=xt[0:16, 0, :]), reads=[xres[0]], chan='ys_out')
            for s0 in pending:
                emit_transposes(ti, s0, l)
                pool_affine(s0)
            if l == DEPTH - 1 and ti + 1 < len(tiles):
                hard_save = S.force_hard
                S.force_hard = (tiles[ti + 1]['kind'] == 's')
                for s_ in range(tiles[ti + 1]['ns']):
                    emit_transposes(ti + 1, s_)
                S.force_hard = hard_save

        n_tl = 0
        for ti, tl in enumerate(tiles):
            S.force_hard = (tl['kind'] == 's')
            for l in range(DEPTH):
                tile_layer(ti, tl, l, n_tl)
                n_tl += 1
            S.force_hard = False

        _nops = int(os.environ.get('KDBG_NOPS', '0'))
        if _nops:
            print('total ops', len(S.ops), 'truncating to', _nops, 'last line', S.ops[min(_nops, len(S.ops)) - 1]['line'])
            S.ops = S.ops[:_nops]
        run_sched(nc, S)
    return nc


def _consts():
    ident = np.eye(128, dtype=np.float32)
    triu = np.triu(np.ones((128, 128), dtype=np.float32))
    invw = np.zeros((128, 2), np.float32)
    invc = np.zeros((128, 2, 15), np.float32)
    for j in range(2):
        for p in range(128):
            w = POOL_WINDOWS[2 * j + p // 64]
            invw[p, j] = 1.0 / w
            for t in range(15):
                invc[p, j, t] = 1.0 / min(w, t + 1)
    return ident, triu, invw, invc


_PROG_CACHE = {}


def run(inputs, nseq, S_len, trace=False):
    key = (nseq, S_len)
    if key not in _PROG_CACHE:
        _PROG_CACHE[key] = build_program(nseq, S_len)
    nc = _PROG_CACHE[key]
    ident, triu, invw, invc = _consts()
    f = lambda a: np.ascontiguousarray(np.asarray(a, dtype=np.float32))
    wnames = ['w_in', 'pool_w', 'pool_scale', 'gmlp_ln_g', 'gmlp_ln_b', 'gmlp_w', 'gmlp_b', 'conv_w', 'conv_b',
              'conv_ln_g', 'conv_ln_b', 'conv_pw_w', 'conv_pw_b', 'w_out', 'b_out', 'ln_g', 'ln_b']
    shared = {k: f(inputs[k]) for k in wnames}
    shared.update(c_ident=ident, c_triu=triu, c_invw=invw, c_invc=invc)
    xp = f(inputs['x_prompt'])
    xs = f(inputs['x_sample'])
    sp_ = f(inputs['state_pool'])
    sc_ = f(inputs['state_conv'])
    in_maps = []
    for c in range(NCORES):
        m = dict(shared)
        m['xp'] = np.ascontiguousarray(xp[c * nseq:(c + 1) * nseq])
        m['xs'] = np.ascontiguousarray(xs[c])
        m['st_pool'] = np.ascontiguousarray(sp_[:, c])
        m['st_conv'] = np.ascontiguousarray(sc_[:, c])
        in_maps.append(m)
    res = run_bass_kernel_spmd(nc, in_maps, core_ids=list(range(NCORES)), **({'trace': True} if trace else {}))
    R = res.results
    y_prompt = np.concatenate([r['yp'] for r in R], axis=0)
    y_sample = np.stack([r['ys'] for r in R], axis=0)
    npp = np.concatenate([r['npp'] for r in R], axis=1)
    ncp = np.concatenate([r['ncp'] for r in R], axis=1)
    nps = np.stack([r['nps'] for r in R], axis=1)
    ncs = np.stack([r['ncs'] for r in R], axis=1)
    nvs = np.stack([r['nvs'] for r in R], axis=1)
    outs = (y_prompt, y_sample, npp, ncp, nps, ncs, nvs)
    return tuple(np.ascontiguousarray(o, dtype=np.float32) for o in outs), res


def kernel(**inputs):
    B, S_len = inputs['x_prompt'].shape[0], inputs['x_prompt'].shape[1]
    outs, _ = run(inputs, B // NCORES, S_len)
    return outs
```

```python
import contextlib
import os
import sys
import numpy as np
import concourse.bass as bass
import concourse.mybir as mybir
from concourse.bass_utils import run_bass_kernel_spmd

F32 = mybir.dt.float32
BF16 = mybir.dt.bfloat16
I32 = mybir.dt.int32
AF = mybir.ActivationFunctionType
ALU = mybir.AluOpType

D_MODEL = 1024
DEPTH = 2
D_POOL = 256
D_GMLP = 384
D_CONV = 384
D_IN = 2816
POOL_WINDOWS = (2, 4, 8, 16)
CONV_W = 31
ALPHA = float((2 * DEPTH) ** 0.25)
LN_EPS = 1e-5
NCORES = 8

COLS = dict(a=(0, 256), ga=(256, 256), u=(512, 384), v=(896, 384), gb=(1280, 384),
            c1=(1664, 384), c2=(2048, 384), gc=(2432, 384))
GROUPS = [('v', 896, 384), ('c2', 2048, 384), ('c1', 1664, 384), ('a', 0, 256), ('ga', 256, 256),
          ('gb', 1280, 384), ('u', 512, 384), ('gc', 2432, 384)]
GOFF = {}
_o = 0
for _n, _c, _w in GROUPS:
    GOFF[_n] = _o
    _o += _w
NG = len(GROUPS)
NRING = 3


class Sched:
    ENG = ('pe', 'act', 'dve', 'pool', 'sp')

    def __init__(self):
        self.ops = []
        self.force_hard = False
        self.defer = False
        self.deferred = []

    def flush(self, k=None):
        k = len(self.deferred) if k is None else min(k, len(self.deferred))
        self.ops.extend(self.deferred[:k])
        del self.deferred[:k]

    def op(self, eng, fn, reads=(), writes=(), chan=None, hard=False, partial=False):
        hard = hard or self.force_hard
        tgt = self.deferred if self.defer else self.ops
        tgt.append(dict(eng=eng, fn=fn, reads=tuple(reads), writes=tuple(writes), chan=chan,
                             hard=hard, partial=partial, idx=-1, line=sys._getframe(1).f_lineno))

    def analyze(self):
        assert not self.deferred
        for i, o in enumerate(self.ops):
            o['idx'] = i
        writers = {}
        readers = {}
        for o in self.ops:
            deps = set()
            for r in o['reads']:
                deps.update(writers.get(r, ()))
            for w in o['writes']:
                if not o['partial']:
                    deps.update(writers.get(w, ()))
                elif writers.get(w) and not self.ops[writers[w][0]]['partial']:
                    deps.add(writers[w][0])
                deps.update(readers.get(w, ()))
            deps.discard(o['idx'])
            o['deps'] = deps
            for r in o['reads']:
                readers.setdefault(r, []).append(o['idx'])
            for w in o['writes']:
                if o['partial']:
                    writers.setdefault(w, []).append(o['idx'])
                else:
                    writers[w] = [o['idx']]
                readers[w] = []
        need = set()
        for o in self.ops:
            for d in o['deps']:
                do = self.ops[d]
                if do['chan'] is not None or do['eng'] != o['eng'] or do['hard']:
                    need.add(d)
        cnt = {e: 0 for e in self.ENG}
        ccnt = {}
        for o in self.ops:
            if o['chan'] is not None:
                ccnt[o['chan']] = ccnt.get(o['chan'], 0) + 16
                o['sig'] = ('c', o['chan'], ccnt[o['chan']])
            elif o['idx'] in need:
                cnt[o['eng']] += 1
                o['sig'] = ('e', o['eng'], cnt[o['eng']])
            else:
                o['sig'] = None
        self.chan_total = ccnt
        self.eng_total = cnt


def run_sched(nc, S, final_wait_eng='sp'):
    S.analyze()
    with contextlib.ExitStack() as st:
        sem_e = {e: st.enter_context(nc.semaphore('s_' + e)) for e in S.ENG}
        sem_c = {c: st.enter_context(nc.semaphore('c_' + str(c))) for c in S.chan_total}
        block = st.enter_context(nc.Block())

        def semof(sig):
            return sem_e[sig[1]] if sig[0] == 'e' else sem_c[sig[1]]

        def gen(engname, engobj):
            waited = {}
            for o in S.ops:
                if o['eng'] != engname:
                    continue
                need_w = {}
                for d in o['deps']:
                    do = S.ops[d]
                    if do['chan'] is None and do['eng'] == engname and not do['hard']:
                        continue
                    sig = do['sig']
                    key = (sig[0], sig[1])
                    if sig[2] > need_w.get(key, 0):
                        need_w[key] = sig[2]
                for key in sorted(need_w):
                    if waited.get(key, 0) >= need_w[key]:
                        continue
                    engobj.wait_ge(semof((key[0], key[1], 0)), need_w[key])
                    waited[key] = need_w[key]
                ins = o['fn'](engobj)
                if o['sig'] is not None:
                    ins.then_inc(semof(o['sig']), 16 if o['sig'][0] == 'c' else 1)
            if engname == final_wait_eng:
                for c, tot in S.chan_total.items():
                    if waited.get(('c', c), 0) < tot:
                        engobj.wait_ge(sem_c[c], tot)
                for e, tot in S.eng_total.items():
                    if e != engname and tot > 0 and waited.get(('e', e), 0) < tot:
                        engobj.wait_ge(sem_e[e], tot)

        @block.tensor
        def _(e):
            gen('pe', e)

        @block.scalar
        def _(e):
            gen('act', e)

        @block.vector
        def _(e):
            gen('dve', e)

        @block.gpsimd
        def _(e):
            gen('pool', e)

        @block.sync
        def _(e):
            gen('sp', e)


def build_program(nseq, S_len, T=512):
    assert S_len % T == 0
    ntile = S_len // T
    nc = bass.Bass("TRN2", target_bir_lowering=False)
    dt_in = lambda name, shape: nc.dram_tensor(name, list(shape), F32, kind="ExternalInput").ap()
    dt_out = lambda name, shape: nc.dram_tensor(name, list(shape), F32, kind="ExternalOutput").ap()
    xp = dt_in("xp", [nseq, S_len, D_MODEL])
    xs = dt_in("xs", [16, D_MODEL])
    st_pool = dt_in("st_pool", [DEPTH, 15, D_POOL])
    st_conv = dt_in("st_conv", [DEPTH, 30, D_CONV])
    w_in = dt_in("w_in", [DEPTH, D_MODEL, D_IN])
    pool_w = dt_in("pool_w", [DEPTH, 4, 64, 64])
    pool_scale = dt_in("pool_scale", [DEPTH, D_POOL])
    gmlp_ln_g = dt_in("gmlp_ln_g", [DEPTH, D_GMLP])
    gmlp_ln_b = dt_in("gmlp_ln_b", [DEPTH, D_GMLP])
    gmlp_w = dt_in("gmlp_w", [DEPTH, 4, 128, 128])
    gmlp_b = dt_in("gmlp_b", [DEPTH, 4, 128])
    conv_w = dt_in("conv_w", [DEPTH, CONV_W, D_CONV])
    conv_b = dt_in("conv_b", [DEPTH, D_CONV])
    conv_ln_g = dt_in("conv_ln_g", [DEPTH, D_CONV])
    conv_ln_b = dt_in("conv_ln_b", [DEPTH, D_CONV])
    conv_pw_w = dt_in("conv_pw_w", [DEPTH, D_CONV, D_CONV])
    conv_pw_b = dt_in("conv_pw_b", [DEPTH, D_CONV])
    w_out = dt_in("w_out", [DEPTH, D_MODEL, D_MODEL])
    b_out = dt_in("b_out", [DEPTH, D_MODEL])
    ln_g = dt_in("ln_g", [DEPTH, D_MODEL])
    ln_b = dt_in("ln_b", [DEPTH, D_MODEL])
    c_ident = dt_in("c_ident", [128, 128])
    c_triu = dt_in("c_triu", [128, 128])
    c_invw = dt_in("c_invw", [128, 2])
    c_invc = dt_in("c_invc", [128, 2, 15])

    yp = dt_out("yp", [nseq, S_len, D_MODEL])
    ys = dt_out("ys", [16, D_MODEL])
    npp = dt_out("npp", [DEPTH, nseq, 15, D_POOL])
    ncp = dt_out("ncp", [DEPTH, nseq, 30, D_CONV])
    nps = dt_out("nps", [DEPTH, 15, D_POOL])
    ncs = dt_out("ncs", [DEPTH, 30, D_CONV])
    nvs = dt_out("nvs", [DEPTH, 16, D_GMLP])

    scr_wout = nc.dram_tensor("scr_wout", [DEPTH, 128, 9 * 1024], BF16, kind="Internal").ap()

    NS = T // 128
    AW = 16 + T
    CW = 30 + T

    with contextlib.ExitStack() as st:
        def sb(name, shape, dt=F32):
            return st.enter_context(nc.sbuf_tensor(name, list(shape), dt))

        def psm(name):
            return st.enter_context(nc.psum_tensor(name, [128, 512], F32))

        x_tok = [sb("x_tok%d" % i, [128, NS, 1024]) for i in range(2)]
        xT = sb("xT", [128, 8, T], BF16)
        mixed = sb("mixed", [128, 9, T], BF16)
        wring = [sb("wring%d" % i, [128, 8 * 384], BF16) for i in range(NRING)]
        wout_sb = sb("wout_sb", [128, 9, 1024], BF16)
        pw_sb = sb("pw_sb", [128, DEPTH, 3, 384], BF16)
        bd_sb = sb("bd_sb", [128, DEPTH, 2, 128], BF16)
        wmt_sb = sb("wmt_sb", [128, DEPTH, 4, 128], BF16)
        D_sb = sb("D_sb", [128, 3, CONV_W, 128], BF16)
        a_ext = [sb("a_ext%d" % l, [128, 2, AW]) for l in range(DEPTH)]
        c_ext = [sb("c_ext%d" % l, [128, 3, CW], BF16) for l in range(DEPTH)]
        tmpA = sb("tmpA", [128, AW])
        tmpB = sb("tmpB", [128, AW])
        dT = sb("dT", [128, 2, T], BF16)
        sga = sb("sga", [128, 2, T])
        vnf = [sb("vnf%d" % i, [128, 384]) for i in range(2)]
        vn = sb("vn", [128, NS, 384], BF16)
        tmpR = [sb("tmpR%d" % i, [128, T]) for i in range(3)]
        ug = [sb("ug%d" % i, [128, T]) for i in range(2)]
        sgb4 = sb("sgb4", [128, T])
        sgc = sb("sgc", [128, 3, T])
        yv = sb("yv", [128, 3, T])
        ysq = sb("ysq", [128, 3, T])
        mean_sb = sb("mean_sb", [128, T])
        rstd_sb = sb("rstd_sb", [128, T])
        ysil = sb("ysil", [128, 3, T], BF16)
        cf = sb("cf", [128, 3, 30])
        lnbc = sb("lnbc", [128, 2, 1024])
        glnbc = sb("glnbc", [128, 2, 384])
        ident = sb("ident", [128, 128])
        epsc = sb("epsc", [128, 1])
        lgb = sb("lgb", [128, DEPTH, 8, 2])
        triu = sb("triu", [128, 128])
        invw = sb("invw", [128, 2])
        invc = sb("invc", [128, 2, 15])
        onesf = sb("onesf", [128, 128])
        onesb = sb("onesb", [128, 128], BF16)
        cw_sb = sb("cw_sb", [128, DEPTH, 3, CONV_W])
        pscale = sb("pscale", [128, DEPTH, 2])
        cvec = sb("cvec", [128, DEPTH, 4, 3])
        hl = sb("hl", [2, 3072], BF16)
        stats = sb("stats", [128, 8, 32])
        vstats = sb("vstats", [128, 4, 16])
        sto = [sb("sto%d" % i, [32, 384]) for i in range(1)]

        banks = [psm("ps%d" % i) for i in range(8)]
        pwst = tmpA[:, 0:256].rearrange("p (l j c) -> p l j c", l=DEPTH, j=2)
        gst = tmpB[:, 0:128]
        sti = tmpR[0][0:32, 0:384]
        S = Sched()
        bank_ctr = [0]

        def nb():
            i = bank_ctr[0] % 8
            bank_ctr[0] += 1
            return banks[i], 'ps%d' % i

        stat_ctr = [0]

        def nstat():
            i = stat_ctr[0] % 8
            stat_ctr[0] += 1
            return stats[:, i:i + 1, :], 'stat%d' % i

        tr_ctr = [0]

        def ntmp():
            i = tr_ctr[0] % 3
            tr_ctr[0] += 1
            return tmpR[i], 'tmpR%d' % i

        sto_ctr = [0]

        S.op('sp', lambda e: e.dma_start(out=ident[:], in_=c_ident[:, :]), writes=['ident'], chan='k_id')
        S.op('sp', lambda e: e.dma_start(out=triu[:], in_=c_triu[:, :]), writes=['triu'], chan='k_tr')
        S.op('sp', lambda e: e.dma_start(out=invw[:], in_=c_invw[:, :]), writes=['invw'], chan='k_iw')
        S.op('sp', lambda e: e.dma_start(out=invc[:], in_=c_invc[:, :, :]), writes=['invc'], chan='k_ic')
        S.op('pool', lambda e: e.memset(onesf[:], 1.0 / 384.0), writes=['onesf'])
        S.op('pool', lambda e: e.memset(onesb[:], 1.0), writes=['onesb'])
        S.op('pool', lambda e: e.memset(epsc[:], LN_EPS), writes=['epsc'])
        S.op('pool', lambda e: e.memset(bd_sb[:], 0.0), writes=['bd'])

        def side_vec():
            def tr_rows(src_tile, nrows, col0, dst_ap_fn, rres, wres, part):
                pb, pbn = nb()
                S.op('pe', (lambda pb: lambda e: e.transpose(out=pb[:, 0:nrows], in_=src_tile[0:nrows, col0:col0 + 128], identity=ident[0:nrows, 0:nrows]))(pb),
                     reads=[rres, 'ident'], writes=[pbn])
                S.op('dve', (lambda pb: lambda e: e.tensor_copy(out=dst_ap_fn(), in_=pb[:, 0:nrows]))(pb), reads=[pbn], writes=[wres], partial=part)

            for l in range(DEPTH):
                st_t, st_n = tmpR[1 + l], 'tmpR%d' % (1 + l)
                S.op('sp', (lambda l, st_t: lambda e: e.dma_start(out=st_t[0:CONV_W, 0:384], in_=conv_w[l, :, :]))(l, st_t), writes=[st_n], chan='k_cw%d' % l)
                for cc in range(3):
                    tr_rows(st_t, CONV_W, cc * 128, (lambda l, cc: lambda: cw_sb[:, l, cc, :])(l, cc), st_n, 'cw', True)
            for vi, vec in enumerate((conv_b, conv_ln_g, conv_ln_b, conv_pw_b)):
                S.op('sp', (lambda vi, vec: lambda e: e.dma_start(out=ug[0][2 * vi:2 * vi + 2, 0:384], in_=vec[:, :]))(vi, vec), writes=['ug0'], chan='k_cv', partial=True)
            for cc in range(3):
                tr_rows(ug[0], 8, cc * 128, (lambda cc: lambda: cvec[:, :, :, cc].rearrange("p l v -> p v l"))(cc), 'ug0', 'cvec', True)
            S.op('sp', lambda e: e.dma_start(out=ug[1][0:2, 0:256], in_=pool_scale[:, :]), writes=['ug1'], chan='k_ps')
            for jc in range(2):
                tr_rows(ug[1], 2, jc * 128, (lambda jc: lambda: pscale[:, :, jc])(jc), 'ug1', 'pscale', True)
            sgaf = sga[:].rearrange("p a b -> p (a b)")
            for vi, vec in enumerate((ln_g, ln_b)):
                S.op('sp', (lambda vi, vec: lambda e: e.dma_start(out=sgaf[2 * vi:2 * vi + 2, 0:1024], in_=vec[:, :]))(vi, vec), writes=['sga'], chan='k_lg', partial=True)
            for kc in range(8):
                tr_rows(sgaf, 4, kc * 128, (lambda kc: lambda: lgb[:, :, kc, :].rearrange("p l v -> p v l"))(kc), 'sga', 'lgb', True)

        def side_pw():
            for l in range(DEPTH):
                for g in range(4):
                    j, r = g // 2, g % 2
                    S.op('sp', (lambda l, g, j, r: lambda e: e.dma_start(out=pwst[64 * r:64 * r + 64, l, j, :], in_=pool_w[l, g, :, :]))(l, g, j, r),
                         writes=['tmpA'], chan='k_pw', partial=True)
            for l in range(DEPTH):
                for g in range(4):
                    j, r = g // 2, g % 2
                    S.op('dve', (lambda l, j, r: lambda e: e.tensor_copy(out=bd_sb[64 * r:64 * r + 64, l, j, 64 * r:64 * r + 64],
                                                                         in_=pwst[64 * r:64 * r + 64, l, j, :]))(l, j, r),
                         reads=['tmpA'], writes=['bd'], partial=True)

        yvf = yv[:].rearrange("p a b -> p (a b)")
        ysqf = ysq[:].rearrange("p a b -> p (a b)")
        ysilf = ysil[:].rearrange("p a b -> p (a b)")
        YR = ['yv0', 'yv1', 'yv2']
        QR = ['ysq0', 'ysq1', 'ysq2']
        LR = ['ysil0', 'ysil1', 'ysil2']
        hl_jobs = []
        for l in range(DEPTH):
            hl_jobs.append((b_out[l, :].unsqueeze(0), 1024, l * 1024))
            hl_jobs.append((gmlp_b[l, :, :].rearrange("h t -> (h t)").unsqueeze(0), 512, 2048 + l * 512))
        def hl_job(jj):
            src, nel, c0 = hl_jobs[jj]
            S.op('sp', (lambda src, nel: lambda e: e.dma_start(out=yvf[0:1, 0:nel], in_=src))(src, nel), writes=YR, chan='k_hl')
            S.op('dve', (lambda nel, c0: lambda e: e.tensor_copy(out=hl[0:1, c0:c0 + nel], in_=yvf[0:1, 0:nel]))(nel, c0),
                 reads=YR, writes=['hl'], hard=True, partial=True)
            S.op('dve', (lambda nel, c0: lambda e: e.tensor_copy(out=ysqf[0:1, 0:nel], in_=hl[0:1, c0:c0 + nel]))(nel, c0), reads=['hl'], writes=QR, hard=True)
            S.op('dve', (lambda nel: lambda e: e.tensor_tensor(out=ysqf[0:1, 0:nel], in0=yvf[0:1, 0:nel], in1=ysqf[0:1, 0:nel], op=ALU.subtract))(nel),
                 reads=YR + QR, writes=QR, hard=True)
            S.op('dve', (lambda nel: lambda e: e.tensor_copy(out=ysilf[0:1, 0:nel], in_=ysqf[0:1, 0:nel]))(nel), reads=QR, writes=LR, hard=True)
            S.op('sp', (lambda nel, c0: lambda e: e.dma_start(out=hl[1:2, c0:c0 + nel], in_=ysilf[0:1, 0:nel]))(nel, c0), reads=LR, writes=['hl'], chan='k_hl2', partial=True)


        def gm_job(jj):
            l, h = jj // 4, jj % 4
            stg_t, stg_n = (yv, 'yv%d' % (jj % 3)) if False else (None, None)
            gsrc = sgc[:, jj // 4, (jj % 4) * 128:(jj % 4) * 128 + 128]
            gres = 'gm_stage%d' % jj
            pb, pbn = nb()
            S.op('sp', (lambda l, h, gsrc: lambda e: e.dma_start(out=gsrc, in_=gmlp_w[l, h, :, :]))(l, h, gsrc), writes=[gres, 'sgc'], chan='k_gst%d' % jj, partial=True)
            S.op('pe', (lambda pb, gsrc: lambda e: e.transpose(out=pb[:, 0:128], in_=gsrc, identity=ident[:]))(pb, gsrc),
                 reads=[gres, 'ident'], writes=[pbn])
            S.op('dve', (lambda l, h, pb: lambda e: e.tensor_tensor(out=wmt_sb[:, l, h, :], in0=pb[:, 0:128], in1=triu[:], op=ALU.mult))(l, h, pb),
                 reads=[pbn, 'triu'], writes=['wmt'], partial=True)


        ALLSCR = ['scr_wout%d' % l_ for l_ in range(DEPTH)]
        tiles = []
        for sq in range(nseq):
            for j in range(ntile):
                tiles.append(dict(kind='p', seq=sq, j=j, T=T, ns=NS, psz=128, first=(j == 0), last=(j == ntile - 1)))
        tiles.append(dict(kind='s', seq=0, j=0, T=16, ns=1, psz=16, first=False, last=True))
        ntl = len(tiles) * DEPTH
        gseq = [(n, g) for n in range(ntl) for g in range(NG)]

        def emit_group_load(q):
            if q >= len(gseq):
                return
            n, g = gseq[q]
            l = n % DEPTH
            gn, gc0, gw = GROUPS[g]
            slot = q % NRING
            S.op('pool', (lambda l, slot, gc0, gw: lambda e: e.dma_start(out=wring[slot][:, 0:8 * gw].rearrange("p (k e) -> p k e", k=8),
                                                                       in_=w_in[l, :, gc0:gc0 + gw].rearrange("(k p) e -> p k e", p=128)))(l, slot, gc0, gw),
                 writes=['wring%d' % slot], chan='ring%d' % slot)

        def emit_x_load(ti):
            if ti >= len(tiles):
                return
            tl = tiles[ti]
            slot = ti % 2
            if tl['kind'] == 'p':
                for s in range(NS):
                    r0 = tl['j'] * T + s * 128
                    S.op('sp', (lambda slot, s, sq, r0: lambda e: e.dma_start(out=x_tok[slot][:, s, :], in_=xp[sq, r0:r0 + 128, :]))(slot, s, tl['seq'], r0),
                         writes=['x_tok%d_%d' % (slot, s)], chan='x%d_%d' % (slot, s))
            else:
                S.op('sp', (lambda slot: lambda e: e.dma_start(out=x_tok[slot][0:16, 0, :], in_=xs[:, :]))(slot),
                     writes=['x_tok%d_0' % slot], chan='x%d_0' % slot)

        def emit_wout_chunk(n, k):
            l = n % DEPTH
            S.op('sp', (lambda l, k: lambda e: e.dma_start(out=wout_sb[:, k, :], in_=scr_wout[l, :, k * 1024:(k + 1) * 1024]))(l, k),
                 reads=ALLSCR, writes=['wout_sb'], chan='wout', partial=(k != 0))

        def emit_lnbc_load(n):
            if n >= ntl:
                return
            l = n % DEPTH
            S.op('sp', (lambda l: lambda e: e.dma_start(out=lnbc[:, 0, :], in_=ln_g[l, :].partition_broadcast(128)))(l), writes=['lnbc'], chan='lnbc')
            S.op('sp', (lambda l: lambda e: e.dma_start(out=lnbc[:, 1, :], in_=ln_b[l, :].partition_broadcast(128)))(l), writes=['lnbc'], chan='lnbc', partial=True)

        def emit_glnbc_load(n):
            if n >= ntl:
                return
            l = n % DEPTH
            S.op('sp', (lambda l: lambda e: e.dma_start(out=glnbc[:, 0, :], in_=gmlp_ln_g[l, :].partition_broadcast(128)))(l), writes=['glnbc'], chan='glnbc')
            S.op('sp', (lambda l: lambda e: e.dma_start(out=glnbc[:, 1, :], in_=gmlp_ln_b[l, :].partition_broadcast(128)))(l), writes=['glnbc'], chan='glnbc', partial=True)

        def emit_D_regen(n, chunks=(0, 1, 2)):
            if n >= ntl:
                return
            l = n % DEPTH
            for cc in chunks:
                S.op('pool', (lambda l, cc: lambda e: e.tensor_tensor(out=D_sb[:, cc, :, :], in0=ident[:].unsqueeze(1).to_broadcast([128, CONV_W, 128]),
                                                                      in1=cw_sb[:, l, cc, :].unsqueeze(2).to_broadcast([128, CONV_W, 128]), op=ALU.mult))(l, cc),
                     reads=['ident', 'cw'], writes=['D%d' % cc])

        def rsqrt_dve(x, y, t, rx, ry, rt, hard):
            xi, yi = x.bitcast(I32), y.bitcast(I32)
            S.op('dve', lambda e: e.tensor_scalar(out=yi, in0=xi, scalar1=1, scalar2=None, op0=ALU.arith_shift_right), reads=rx, writes=ry, hard=hard)
            S.op('dve', lambda e: e.tensor_scalar(out=yi, in0=yi, scalar1=-1, scalar2=0x5f3759df, op0=ALU.mult, op1=ALU.add), reads=ry, writes=ry, hard=hard)
            for _ in range(3):
                S.op('dve', lambda e: e.tensor_tensor(out=t, in0=y, in1=y, op=ALU.mult), reads=ry, writes=rt, hard=hard)
                S.op('dve', lambda e: e.tensor_tensor(out=t, in0=t, in1=x, op=ALU.mult), reads=rt + rx, writes=rt, hard=hard)
                S.op('dve', lambda e: e.tensor_scalar(out=t, in0=t, scalar1=-0.5, scalar2=1.5, op0=ALU.mult, op1=ALU.add), reads=rt, writes=rt, hard=hard)
                S.op('dve', lambda e: e.tensor_tensor(out=y, in0=y, in1=t, op=ALU.mult), reads=ry + rt, writes=ry, hard=hard)

        def ln_small(psz, mv, nm, ncol=1):
            S.op('act', lambda e: e.activation(out=mv[0:psz, :, 3:4], in_=mv[0:psz, :, 1:2], func=AF.Sqrt, bias=epsc[0:psz, 0:1], scale=1.0),
                 reads=[nm, 'epsc'], writes=[nm], hard=True)
            S.op('dve', lambda e: e.reciprocal(out=mv[0:psz, :, 2:3], in_=mv[0:psz, :, 3:4]), reads=[nm], writes=[nm], hard=True)

        def state_out(src_fn, nch, ncol, dst_ap, reads):
            i = 0
            sto_ctr[0] += 1
            pb, pbn = nb()
            for ch in range(nch):
                S.op('pe', (lambda ch, pb: lambda e: e.transpose(out=pb[0:ncol, ch * 128:(ch + 1) * 128], in_=src_fn(ch), identity=ident[:]))(ch, pb),
                     reads=list(reads) + ['ident'], writes=[pbn])
            S.op('act', (lambda i, pb: lambda e: e.copy(out=sto[i][0:ncol, 0:nch * 128], in_=pb[0:ncol, 0:nch * 128]))(i, pb),
                 reads=[pbn], writes=['sto%d' % i])
            S.op('sp', (lambda i: lambda e: e.dma_start(out=dst_ap, in_=sto[i][0:ncol, 0:nch * 128]))(i),
                 reads=['sto%d' % i], chan='sto%d' % i)

        def emit_transposes(ti, s, affine_l=None):
            tl = tiles[ti]
            psz = tl['psz']
            slot = ti % 2
            xt = x_tok[slot]
            xr = 'x_tok%d_%d' % (slot, s)
            for hb in range(2):
                pb, pbn = nb()
                for q in range(4):
                    kc = hb * 4 + q
                    S.op('pe', (lambda kc, q, pb: lambda e: e.transpose(out=pb[:, q * 128:q * 128 + psz], in_=xt[0:psz, s, kc * 128:(kc + 1) * 128],
                                                                        identity=ident[0:psz, 0:psz]))(kc, q, pb),
                         reads=[xr, 'ident'], writes=[pbn])
                if affine_l is None:
                    S.op('act', (lambda hb, pb: lambda e: e.copy(out=xT[:, hb * 4:hb * 4 + 4, s * 128:s * 128 + psz],
                                                                 in_=pb[:].rearrange("p (q t) -> p q t", q=4)[:, :, 0:psz]))(hb, pb),
                         reads=[pbn], writes=['xT_%d' % s])
                else:
                    for q in range(4):
                        kc = hb * 4 + q
                        S.op('act', (lambda kc, q, pb: lambda e: e.activation(out=xT[:, kc, s * 128:s * 128 + psz], in_=pb[:, q * 128:q * 128 + psz], func=AF.Identity,
                                                                              bias=lgb[:, affine_l, kc, 1:2], scale=lgb[:, affine_l, kc, 0:1]))(kc, q, pb),
                             reads=[pbn, 'lgb'], writes=['xT_%d' % s], partial=(q != 0 or hb != 0))

        emit_x_load(0)
        for s_ in range(tiles[0]['ns']):
            emit_transposes(0, s_)
        emit_glnbc_load(0)
        side_vec()
        for jj in range(8):
            gm_job(jj)
        side_pw()
        for jj in range(len(hl_jobs)):
            hl_job(jj)
        WO_ROWS = [(0, 128), (128, 128), (256, 96), (352, 96), (448, 96), (544, 96), (640, 128), (768, 128), (896, 128)]
        for l in range(DEPTH):
            for ch, (r0, rn) in enumerate(WO_ROWS):
                S.op('pool', (lambda l, ch, r0, rn: lambda e: e.dma_start(out=scr_wout[l, :, ch * 1024:(ch + 1) * 1024], in_=w_out[l, r0:r0 + 128, :]))(l, ch, r0, rn),
                     writes=['scr_wout%d' % l], chan='k_wo%d' % l, partial=True)
        for l in range(DEPTH):
            S.op('pool', (lambda l: lambda e: e.dma_start(out=pw_sb[:, l, :, :], in_=conv_pw_w[l, :, :].rearrange("(k p) e -> p k e", p=128)))(l),
                 writes=['pw_sb'], chan='k_pwc', partial=True)
        for l in range(DEPTH):
            S.op('sp', (lambda l: lambda e: e.dma_start(out=scr_wout[l, 96:98, 2048:3072], in_=hl[0:2, l * 1024:(l + 1) * 1024]))(l),
                 reads=['hl'], writes=['scr_wout%d' % l], chan='k_bo2_%d' % l)
        for q in range(T // 128):
            S.op('sp', (lambda q: lambda e: e.dma_start(out=mixed[96:98, 2, q * 128:(q + 1) * 128], in_=onesb[0:2, 0:128]))(q),
                 reads=['onesb'], writes=['mixed_b'], chan='k_ones', partial=True)
        qload = [0]
        for _ in range(NRING):
            emit_group_load(qload[0])
            qload[0] += 1
        emit_D_regen(0)

        def group_ready(q):
            return wring[q % NRING], 'wring%d' % (q % NRING)

        def tile_layer(ti, tl, l, n):
            Tt, ns, psz = tl['T'], tl['ns'], tl['psz']
            slot = ti % 2
            xt = x_tok[slot]
            xres = ['x_tok%d_%d' % (slot, s) for s in range(ns)]
            ae, ce = a_ext[l], c_ext[l]
            aen, cen = 'a_ext%d' % l, 'c_ext%d' % l
            if tl['kind'] == 'p' and tl['first']:
                S.op('pool', (lambda ae: lambda e: e.memset(ae[:, :, 0:16], 0.0))(ae), writes=[aen])
                S.op('pool', (lambda ce: lambda e: e.memset(ce[:, :, 0:30], 0.0))(ce), writes=[cen])
            elif tl['kind'] == 's':
                S.op('sp', (lambda l: lambda e: e.dma_start(out=sti[0:15, 0:256], in_=st_pool[l, :, :]))(l), writes=['tmpR0'], chan='sti')
                pb, pbn = nb()
                for jc in range(2):
                    S.op('pe', (lambda jc, pb: lambda e: e.transpose(out=pb[:, jc * 16:jc * 16 + 15], in_=sti[0:15, jc * 128:(jc + 1) * 128], identity=ident[0:15, 0:15]))(jc, pb),
                         reads=['tmpR0', 'ident'], writes=[pbn])
                S.op('dve', (lambda ae, pb: lambda e: e.tensor_copy(out=ae[:, :, 1:16], in_=pb[:, 0:32].rearrange("p (j c) -> p j c", j=2)[:, :, 0:15]))(ae, pb),
                     reads=[pbn], writes=[aen])
                S.op('sp', (lambda l: lambda e: e.dma_start(out=sti[0:30, 0:384], in_=st_conv[l, :, :]))(l), reads=[], writes=['tmpR0'], chan='sti')
                pb2, pbn2 = nb()
                for cc in range(3):
                    S.op('pe', (lambda cc, pb2: lambda e: e.transpose(out=pb2[:, cc * 32:cc * 32 + 30], in_=sti[0:30, cc * 128:(cc + 1) * 128], identity=ident[0:30, 0:30]))(cc, pb2),
                         reads=['tmpR0', 'ident'], writes=[pbn2])
                S.op('dve', (lambda ce, pb2: lambda e: e.tensor_copy(out=ce[:, :, 0:30], in_=pb2[:, 0:96].rearrange("p (j c) -> p j c", j=3)[:, :, 0:30]))(ce, pb2),
                     reads=[pbn2], writes=[cen])
                S.op('sp', (lambda l: lambda e: e.dma_start(out=ncs[l, 0:14, :], in_=st_conv[l, 16:30, :]))(l), chan='ncs_cp')
            else:
                S.op('pool', (lambda ae: lambda e: e.tensor_copy(out=ae[:, :, 1:16], in_=ae[:, :, T + 1:T + 16]))(ae), reads=[aen], writes=[aen])
                S.op('pool', (lambda ce: lambda e: e.tensor_copy(out=ce[:, :, 0:30], in_=ce[:, :, T:T + 30]))(ce), reads=[cen], writes=[cen])

            xTres = ['xT_%d' % s for s in range(ns)]

            qbase = n * NG

            def use_group(gidx):
                q = qbase + gidx
                return group_ready(q)

            def done_group(gidx):
                emit_group_load(qload[0])
                qload[0] += 1
                emit_wout_chunk(n, gidx)
                if gidx == 7:
                    emit_wout_chunk(n, 8)
                if gidx == 2 and l == DEPTH - 1:
                    emit_x_load(ti + 1)
                if gidx == 4:
                    emit_lnbc_load(n)
                if gidx == 5:
                    emit_glnbc_load(n + 1)

            wt, wres = use_group(0)
            wv = wt[:, 0:8 * 384].rearrange("p (k e) -> p k e", k=8)
            vps = []
            for s in range(ns):
                pb, pbn = nb()
                for kc in range(8):
                    S.op('pe', (lambda s, kc, pb, wv: lambda e: e.matmul(pb[0:psz, 0:384], lhsT=xT[:, kc, s * 128:s * 128 + psz], rhs=wv[:, kc, :],
                                                                         start=(kc == 0), stop=(kc == 7)))(s, kc, pb, wv),
                         reads=[xTres[s], wres], writes=[pbn])
                S.op('dve', (lambda s, pb: lambda e: e.bn_stats(out=vstats[0:psz, s, 8:14], in_=pb[0:psz, 0:384]))(s, pb), reads=[pbn], writes=['vstats'], hard=True, partial=(s != 0))
                S.op('dve', (lambda s: lambda e: e.bn_aggr(out=vstats[0:psz, s, 0:2], in_=vstats[0:psz, s, 8:14]))(s), reads=['vstats'], writes=['vstats'], hard=True, partial=True)
                vps.append((pb, pbn))
            ln_small(psz, vstats[:, 0:ns, :], 'vstats', ns)
            for s in range(ns):
                pb, pbn = vps[s]
                vf = vnf[s % 2]
                vfn = 'vnf%d' % (s % 2)
                S.op('dve', (lambda s, pb, vf: lambda e: e.tensor_scalar(out=vf[0:psz, :], in0=pb[0:psz, 0:384], scalar1=vstats[0:psz, s, 0:1], scalar2=vstats[0:psz, s, 2:3],
                                                                         op0=ALU.subtract, op1=ALU.mult))(s, pb, vf),
                     reads=[pbn, 'vstats'], writes=[vfn])
                S.op('dve', (lambda vf: lambda e: e.tensor_tensor(out=vf[0:psz, :], in0=vf[0:psz, :], in1=glnbc[0:psz, 0, :], op=ALU.mult))(vf),
                     reads=[vfn, 'glnbc'], writes=[vfn])
                if tl['kind'] == 's':
                    S.op('dve', (lambda vf: lambda e: e.tensor_tensor(out=vf[0:psz, :], in0=vf[0:psz, :], in1=glnbc[0:psz, 1, :], op=ALU.add))(vf),
                         reads=[vfn, 'glnbc'], writes=[vfn])
                    S.op('dve', (lambda s, vf: lambda e: e.tensor_copy(out=vn[0:psz, s, :], in_=vf[0:psz, :]))(s, vf), reads=[vfn], writes=['vn_%d' % s])
                    S.op('sp', (lambda l, vf: lambda e: e.dma_start(out=nvs[l, :, :], in_=vf[0:16, :]))(l, vf), reads=[vfn], chan='nvs')
                else:
                    S.op('dve', (lambda s, vf: lambda e: e.tensor_tensor(out=vn[0:psz, s, :], in0=vf[0:psz, :], in1=glnbc[0:psz, 1, :], op=ALU.add))(s, vf),
                         reads=[vfn, 'glnbc'], writes=['vn_%d' % s])
            done_group(0)

            wt2, wres2 = use_group(1)
            wc2 = wt2[:, 0:8 * 384].rearrange("p (k e) -> p k e", k=8)
            sigs = []
            for cc in range(3):
                pb, pbn = nb()
                for kc in range(8):
                    S.op('pe', (lambda cc, kc, pb, wc2: lambda e: e.matmul(pb[:, 0:Tt], lhsT=wc2[:, kc, cc * 128:(cc + 1) * 128], rhs=xT[:, kc, 0:Tt],
                                                                           start=(kc == 0), stop=(kc == 7)))(cc, kc, pb, wc2),
                         reads=xTres + [wres2], writes=[pbn])
                tt, ttn = ntmp()
                S.op('act', (lambda pb, tt: lambda e: e.activation(out=tt[:, 0:Tt], in_=pb[:, 0:Tt], func=AF.Sigmoid))(pb, tt), reads=[pbn], writes=[ttn])
                sigs.append((tt, ttn))
            done_group(1)
            wt3, wres3 = use_group(2)
            wc1 = wt3[:, 0:8 * 384].rearrange("p (k e) -> p k e", k=8)
            ntail = min(30, Tt)
            for cc in range(3):
                pb, pbn = nb()
                for kc in range(8):
                    S.op('pe', (lambda cc, kc, pb, wc1: lambda e: e.matmul(pb[:, 0:Tt], lhsT=wc1[:, kc, cc * 128:(cc + 1) * 128], rhs=xT[:, kc, 0:Tt],
                                                                           start=(kc == 0), stop=(kc == 7)))(cc, kc, pb, wc1),
                         reads=xTres + [wres3], writes=[pbn])
                tt, ttn = sigs[cc]
                S.op('dve', (lambda cc, pb, tt, ce: lambda e: e.tensor_tensor(out=ce[:, cc, 30:30 + Tt], in0=pb[:, 0:Tt], in1=tt[:, 0:Tt], op=ALU.mult))(cc, pb, tt, ce),
                     reads=[pbn, ttn], writes=[cen], partial=True)
                if tl['last']:
                    S.op('dve', (lambda cc, pb, tt: lambda e: e.tensor_tensor(out=cf[:, cc, 0:ntail], in0=pb[:, Tt - ntail:Tt], in1=tt[:, Tt - ntail:Tt], op=ALU.mult))(cc, pb, tt),
                         reads=[pbn, ttn], writes=['cf'], partial=(cc != 0))
            done_group(2)
            if tl['last']:
                if tl['kind'] == 'p':
                    state_out(lambda ch: cf[:, ch, 0:30], 3, 30, ncp[l, tl['seq'], :, :], ['cf'])
                else:
                    state_out(lambda ch: cf[:, ch, 0:16], 3, 16, ncs[l, 14:30, :], ['cf'])

            wt4, wres4 = use_group(3)
            wa = wt4[:, 0:8 * 256].rearrange("p (k e) -> p k e", k=8)
            for jc in range(2):
                pb, pbn = nb()
                for kc in range(8):
                    S.op('pe', (lambda jc, kc, pb, wa: lambda e: e.matmul(pb[:, 0:Tt], lhsT=wa[:, kc, jc * 128:(jc + 1) * 128], rhs=xT[:, kc, 0:Tt],
                                                                          start=(kc == 0), stop=(kc == 7)))(jc, kc, pb, wa),
                         reads=xTres + [wres4], writes=[pbn])
                S.op('act', (lambda jc, pb, ae: lambda e: e.copy(out=ae[:, jc, 16:16 + Tt], in_=pb[:, 0:Tt]))(jc, pb, ae), reads=[pbn], writes=[aen], partial=True)
            done_group(3)
            W = 16 + Tt
            for jc in range(2):
                a_ = ae[:, jc, :]
                nlev = 2 if jc == 0 else 4
                src = a_
                bufs = [tmpA, tmpB]
                bi = 0
                for lev in range(1, nlev + 1):
                    sh = 1 << (lev - 1)
                    lo = (1 << lev)
                    dst = bufs[bi]
                    full = (lev < nlev)
                    p0 = 0 if full else 64
                    S.op('pool', (lambda src, dst, sh, lo, p0, W: lambda e: e.tensor_tensor(out=dst[p0:128, lo:W], in0=src[p0:128, lo:W], in1=src[p0:128, lo - sh:W - sh], op=ALU.add))(src, dst, sh, lo, p0, W),
                         reads=[aen, 'tmpA', 'tmpB'], writes=['tmpA' if bi == 0 else 'tmpB'])
                    if lev == nlev - 1:
                        low_src = dst
                    src = dst
                    bi ^= 1
                hi_src = src
                for (p0, p1, sr) in ((0, 64, low_src), (64, 128, hi_src)):
                    S.op('dve', (lambda jc, p0, p1, sr, a_: lambda e: e.scalar_tensor_tensor(out=dT[p0:p1, jc, 0:Tt], in0=sr[p0:p1, 16:16 + Tt], scalar=invw[p0:p1, jc:jc + 1],
                                                                                          in1=a_[p0:p1, 16:16 + Tt], op0=ALU.mult, op1=ALU.subtract))(jc, p0, p1, sr, a_),
                         reads=[aen, 'tmpA', 'tmpB', 'invw'], writes=['dT'], partial=not (jc == 0 and p0 == 0))
                    if tl['kind'] == 'p' and tl['first']:
                        S.op('pool', (lambda jc, p0, p1, sr: lambda e: e.tensor_tensor(out=cf[p0:p1, 0, 0:15], in0=sr[p0:p1, 16:31], in1=invc[p0:p1, jc, :], op=ALU.mult))(jc, p0, p1, sr),
                             reads=['tmpA', 'tmpB', 'invc', 'cf'], writes=['cf'], hard=True)
                        S.op('pool', (lambda jc, p0, p1, a_: lambda e: e.tensor_tensor(out=dT[p0:p1, jc, 0:15], in0=cf[p0:p1, 0, 0:15], in1=a_[p0:p1, 16:31], op=ALU.subtract))(jc, p0, p1, a_),
                             reads=['cf', aen, 'dT'], writes=['dT'], partial=True)
            if tl['last']:
                if tl['kind'] == 'p':
                    state_out(lambda ch: ae[:, ch, Tt + 1:Tt + 16], 2, 15, npp[l, tl['seq'], :, :], [aen])
                else:
                    state_out(lambda ch: ae[:, ch, Tt + 1:Tt + 16], 2, 15, nps[l, :, :], [aen])
            for cc in range(3):
                pb, pbn = nb()
                for k in range(CONV_W):
                    S.op('pe', (lambda cc, k, pb, ce: lambda e: e.matmul(pb[:, 0:Tt], lhsT=D_sb[:, cc, k, :], rhs=ce[:, cc, k:k + Tt], start=(k == 0), stop=(k == CONV_W - 1)))(cc, k, pb, ce),
                         reads=[cen, 'D%d' % cc], writes=[pbn])
                S.op('act', (lambda cc, pb: lambda e: e.activation(out=yv[:, cc, 0:Tt], in_=pb[:, 0:Tt], func=AF.Identity, bias=cvec[:, l, 0, cc:cc + 1], scale=1.0))(cc, pb),
                     reads=[pbn, 'cvec'], writes=['yv%d' % cc])
                S.op('act', (lambda cc, pb: lambda e: e.activation(out=ysq[:, cc, 0:Tt], in_=pb[:, 0:Tt], func=AF.Square, bias=cvec[:, l, 0, cc:cc + 1], scale=1.0))(cc, pb),
                     reads=[pbn, 'cvec'], writes=['ysq%d' % cc])
                emit_D_regen(n + 1, (cc,))
            pm, pmn = nb()
            for cc in range(3):
                S.op('pe', (lambda cc, pm: lambda e: e.matmul(pm[:, 0:Tt], lhsT=onesf[:], rhs=yv[:, cc, 0:Tt], start=(cc == 0), stop=(cc == 2)))(cc, pm),
                     reads=['yv%d' % cc, 'onesf'], writes=[pmn])
            pq, pqn = nb()
            for cc in range(3):
                S.op('pe', (lambda cc, pq: lambda e: e.matmul(pq[:, 0:Tt], lhsT=onesf[:], rhs=ysq[:, cc, 0:Tt], start=(cc == 0), stop=(cc == 2)))(cc, pq),
                     reads=['ysq%d' % cc, 'onesf'], writes=[pqn])
            S.op('act', (lambda pm: lambda e: e.copy(out=mean_sb[:, 0:Tt], in_=pm[:, 0:Tt]))(pm), reads=[pmn], writes=['mean_sb'])
            for cc in range(3):
                S.op('pool', (lambda cc: lambda e: e.tensor_tensor(out=yv[:, cc, 0:Tt], in0=yv[:, cc, 0:Tt], in1=mean_sb[:, 0:Tt], op=ALU.subtract))(cc),
                     reads=['yv%d' % cc, 'mean_sb'], writes=['yv%d' % cc])
            cvt = ysq[:, 0, 0:Tt]
            cvt2 = ysq[:, 1, 0:Tt]
            S.op('dve', (lambda cvt: lambda e: e.tensor_tensor(out=cvt, in0=mean_sb[:, 0:Tt], in1=mean_sb[:, 0:Tt], op=ALU.mult))(cvt), reads=['mean_sb'], writes=['ysq0'])
            S.op('dve', (lambda pq, cvt: lambda e: e.tensor_tensor(out=cvt, in0=pq[:, 0:Tt], in1=cvt, op=ALU.subtract))(pq, cvt), reads=[pqn, 'ysq0'], writes=['ysq0'])
            S.op('act', (lambda cvt, cvt2: lambda e: e.activation(out=cvt2, in_=cvt, func=AF.Sqrt, bias=epsc[:, 0:1], scale=1.0))(cvt, cvt2), reads=['ysq0', 'epsc'], writes=['ysq1'])
            S.op('dve', (lambda cvt2: lambda e: e.reciprocal(out=rstd_sb[:, 0:Tt], in_=cvt2))(cvt2), reads=['ysq1'], writes=['rstd_sb'])
            wt4b, wres4b = use_group(4)
            wga = wt4b[:, 0:8 * 256].rearrange("p (k e) -> p k e", k=8)
            for jc in range(2):
                pb, pbn = nb()
                for kc in range(8):
                    S.op('pe', (lambda jc, kc, pb, wga: lambda e: e.matmul(pb[:, 0:Tt], lhsT=wga[:, kc, jc * 128:(jc + 1) * 128], rhs=xT[:, kc, 0:Tt],
                                                                           start=(kc == 0), stop=(kc == 7)))(jc, kc, pb, wga),
                         reads=xTres + [wres4b], writes=[pbn])
                S.op('act', (lambda jc, pb: lambda e: e.activation(out=sga[:, jc, 0:Tt], in_=pb[:, 0:Tt], func=AF.Silu))(jc, pb), reads=[pbn], writes=['sga'], partial=(jc != 0))
            done_group(4)
            wt5, wres5 = use_group(5)
            wgb = wt5[:, 0:8 * 384].rearrange("p (k e) -> p k e", k=8)
            sgbs = []
            for h in range(4):
                pb, pbn = nb()
                for kc in range(8):
                    S.op('pe', (lambda h, kc, pb, wgb: lambda e: e.matmul(pb[0:96, 0:Tt], lhsT=wgb[:, kc, h * 96:(h + 1) * 96], rhs=xT[:, kc, 0:Tt],
                                                                          start=(kc == 0), stop=(kc == 7)))(h, kc, pb, wgb),
                         reads=xTres + [wres5], writes=[pbn])
                if h < 3:
                    tt, ttn = ntmp()
                else:
                    tt, ttn = sgb4, 'sgb4'
                S.op('act', (lambda pb, tt: lambda e: e.activation(out=tt[0:96, 0:Tt], in_=pb[0:96, 0:Tt], func=AF.Silu))(pb, tt), reads=[pbn], writes=[ttn])
                sgbs.append((tt, ttn))
            done_group(5)
            wt6, wres6 = use_group(6)
            wu = wt6[:, 0:8 * 384].rearrange("p (k e) -> p k e", k=8)
            for h in range(4):
                pb, pbn = nb()
                for kc in range(8):
                    S.op('pe', (lambda h, kc, pb, wu: lambda e: e.matmul(pb[0:96, 0:Tt], lhsT=wu[:, kc, h * 96:(h + 1) * 96], rhs=xT[:, kc, 0:Tt],
                                                                         start=(kc == 0), stop=(kc == 7)))(h, kc, pb, wu),
                         reads=xTres + [wres6], writes=[pbn])
                tt, ttn = sgbs[h]
                ugt, ugn = ug[h % 2], 'ug%d' % (h % 2)
                S.op('dve', (lambda pb, tt, ugt: lambda e: e.tensor_tensor(out=ugt[0:96, 0:Tt], in0=pb[0:96, 0:Tt], in1=tt[0:96, 0:Tt], op=ALU.mult))(pb, tt, ugt),
                     reads=[pbn, ttn], writes=[ugn])
                pz, pzn = nb()
                o0 = 2048 + l * 512 + h * 128
                S.op('pe', (lambda pz, o0: lambda e: e.matmul(pz[0:96, 0:ns * 128].rearrange("p (s t) -> p s t", s=ns)[:, :, 0:psz], lhsT=onesb[0:2, 0:96],
                                                              rhs=hl[0:2, o0:o0 + psz].unsqueeze(1).to_broadcast([2, ns, psz]), start=True, stop=False))(pz, o0),
                     reads=['onesb', 'hl'], writes=[pzn])
                for s in range(ns):
                    S.op('pe', (lambda h, s, pz: lambda e: e.matmul(pz[0:96, s * 128:s * 128 + psz], lhsT=vn[0:psz, s, h * 96:(h + 1) * 96], rhs=wmt_sb[0:psz, l, h, 0:psz],
                                                                    start=False, stop=(s == ns - 1)))(h, s, pz),
                         reads=['vn_%d' % s, 'wmt'], writes=[pzn])
                S.op('dve', (lambda h, pz, ugt: lambda e: e.tensor_tensor(out=mixed[0:96, 2 + h, 0:Tt], in0=pz[0:96, 0:Tt], in1=ugt[0:96, 0:Tt], op=ALU.mult))(h, pz, ugt),
                     reads=[pzn, ugn], writes=['mixed_b'], partial=(h != 0))
                S.flush(3)
            done_group(6)

            wt7, wres7 = use_group(7)
            wgc = wt7[:, 0:8 * 384].rearrange("p (k e) -> p k e", k=8)
            for cc in range(3):
                pb, pbn = nb()
                for kc in range(8):
                    S.op('pe', (lambda cc, kc, pb, wgc: lambda e: e.matmul(pb[:, 0:Tt], lhsT=wgc[:, kc, cc * 128:(cc + 1) * 128], rhs=xT[:, kc, 0:Tt],
                                                                           start=(kc == 0), stop=(kc == 7)))(cc, kc, pb, wgc),
                         reads=xTres + [wres7], writes=[pbn])
                S.op('act', (lambda cc, pb: lambda e: e.activation(out=sgc[:, cc, 0:Tt], in_=pb[:, 0:Tt], func=AF.Silu))(cc, pb), reads=[pbn], writes=['sgc'], partial=(cc != 0))
                S.flush(2)
            done_group(7)

            S.flush()
            for cc in range(3):
                S.op('dve', (lambda cc: lambda e: e.tensor_tensor(out=yv[:, cc, 0:Tt], in0=yv[:, cc, 0:Tt], in1=rstd_sb[:, 0:Tt], op=ALU.mult))(cc),
                     reads=['yv%d' % cc, 'rstd_sb'], writes=['yv%d' % cc])
                S.op('act', (lambda cc: lambda e: e.activation(out=ysil[:, cc, 0:Tt], in_=yv[:, cc, 0:Tt], func=AF.Silu, bias=cvec[:, l, 2, cc:cc + 1], scale=cvec[:, l, 1, cc:cc + 1]))(cc),
                     reads=['yv%d' % cc, 'cvec'], writes=['ysil%d' % cc])
            for jc in range(2):
                pb, pbn = nb()
                S.op('pe', (lambda jc, pb: lambda e: e.matmul(pb[:, 0:Tt], lhsT=bd_sb[:, l, jc, :], rhs=dT[:, jc, 0:Tt], start=True, stop=True))(jc, pb),
                     reads=['dT', 'bd'], writes=[pbn])
                S.op('dve', (lambda jc, pb: lambda e: e.scalar_tensor_tensor(out=mixed[:, jc, 0:Tt], in0=pb[:, 0:Tt], scalar=pscale[:, l, jc:jc + 1], in1=sga[:, jc, 0:Tt],
                                                                             op0=ALU.mult, op1=ALU.mult))(jc, pb),
                     reads=[pbn, 'pscale', 'sga'], writes=['mixed_a'], partial=(jc != 0))

            for eo in range(3):
                pb, pbn = nb()
                for kc in range(3):
                    S.op('pe', (lambda eo, kc, pb: lambda e: e.matmul(pb[:, 0:Tt], lhsT=pw_sb[:, l, kc, eo * 128:(eo + 1) * 128], rhs=ysil[:, kc, 0:Tt], start=(kc == 0), stop=(kc == 2)))(eo, kc, pb),
                         reads=['ysil%d' % kc, 'pw_sb'], writes=[pbn])
                S.op('dve', (lambda eo, pb: lambda e: e.scalar_tensor_tensor(out=mixed[:, 6 + eo, 0:Tt], in0=pb[:, 0:Tt], scalar=cvec[:, l, 3, eo:eo + 1], in1=sgc[:, eo, 0:Tt],
                                                                             op0=ALU.add, op1=ALU.mult))(eo, pb),
                     reads=[pbn, 'cvec', 'sgc'], writes=['mixed_c'], partial=(eo != 0))
            pending = []

            def pool_affine(s):
                S.op('pool', (lambda s: lambda e: e.tensor_tensor(out=xt[0:psz, s, :], in0=xt[0:psz, s, :], in1=lnbc[0:psz, 0, :], op=ALU.mult))(s),
                     reads=[xres[s], 'lnbc'], writes=[xres[s]])
                S.op('pool', (lambda s: lambda e: e.tensor_tensor(out=xt[0:psz, s, :], in0=xt[0:psz, s, :], in1=lnbc[0:psz, 1, :], op=ALU.add))(s),
                     reads=[xres[s], 'lnbc'], writes=[xres[s]])

            KP = [128, 128, 98, 96, 96, 96, 128, 128, 128]
            mres = ['mixed_a', 'mixed_b', 'mixed_c']
            for s in range(ns):
                pos = []
                for hf in range(2):
                    pb, pbn = nb()
                    for kc in range(9):
                        kp = KP[kc]
                        S.op('pe', (lambda s, hf, kc, kp, pb: lambda e: e.matmul(pb[0:psz, 0:512], lhsT=mixed[0:kp, kc, s * 128:s * 128 + psz], rhs=wout_sb[0:kp, kc, hf * 512:(hf + 1) * 512],
                                                                                  start=(kc == 0), stop=(kc == 8)))(s, hf, kc, kp, pb),
                             reads=mres + ['wout_sb'], writes=[pbn])
                    pos.append((pb, pbn))
                stt, stn = nstat()
                for hf in range(2):
                    pb, pbn = pos[hf]
                    S.op('dve', (lambda s, hf, pb: lambda e: e.scalar_tensor_tensor(out=xt[0:psz, s, hf * 512:(hf + 1) * 512], in0=xt[0:psz, s, hf * 512:(hf + 1) * 512], scalar=ALPHA,
                                                                                    in1=pb[0:psz, 0:512], op0=ALU.mult, op1=ALU.add))(s, hf, pb),
                         reads=[pbn, xres[s]], writes=[xres[s]])
                    S.op('dve', (lambda s, hf, stt: lambda e: e.bn_stats(out=stt[0:psz, 0, 8 + 6 * hf:14 + 6 * hf], in_=xt[0:psz, s, hf * 512:(hf + 1) * 512]))(s, hf, stt),
                         reads=[xres[s]], writes=[stn], hard=True, partial=(hf == 1))
                S.op('dve', (lambda stt: lambda e: e.bn_aggr(out=stt[0:psz, 0, 0:2], in_=stt[0:psz, 0, 8:20]))(stt), reads=[stn], writes=[stn], hard=True)
                ln_small(psz, stt, stn)
                S.op('dve', (lambda stt: lambda e: e.tensor_scalar(out=stt[0:psz, 0, 5:6], in0=stt[0:psz, 0, 0:1], scalar1=stt[0:psz, 0, 2:3], scalar2=-1.0,
                                                                   op0=ALU.mult, op1=ALU.mult))(stt), reads=[stn], writes=[stn], hard=True)
                S.op('act', (lambda s, stt: lambda e: e.activation(out=xt[0:psz, s, :], in_=xt[0:psz, s, :], func=AF.Identity, bias=stt[0:psz, 0, 5:6], scale=stt[0:psz, 0, 2:3]))(s, stt),
                     reads=[xres[s], stn], writes=[xres[s]])
                if l == DEPTH - 1:
                    pool_affine(s)
                else:
                    pending.append(s)
                    if len(pending) > 1:
                        s0 = pending.pop(0)
                        emit_transposes(ti, s0, l)
                        pool_affine(s0)
                if l == DEPTH - 1:
                    if tl['kind'] == 'p':
                        r0 = tl['j'] * T + s * 128
                        S.op('pool', (lambda s, sq, r0: lambda e: e.dma_start(out=yp[sq, r0:r0 + 128, :], in_=xt[:, s, :]))(s, tl['seq'], r0),
                             reads=[xres[s]], chan='yo%d_%d' % (slot, s))
                    else:
                        S.op('sp', lambda e: e.dma_start(out=ys[:, :], in_=xt[0:16, 0, :]), reads=[xres[0]], chan='ys_out')
            for s0 in pending:
                emit_transposes(ti, s0, l)
                pool_affine(s0)
            if l == DEPTH - 1 and ti + 1 < len(tiles):
                hard_save = S.force_hard
                S.force_hard = (tiles[ti + 1]['kind'] == 's')
                for s_ in range(tiles[ti + 1]['ns']):
                    emit_transposes(ti + 1, s_)
                S.force_hard = hard_save

        n_tl = 0
        for ti, tl in enumerate(tiles):
            S.force_hard = (tl['kind'] == 's')
            for l in range(DEPTH):
                tile_layer(ti, tl, l, n_tl)
                n_tl += 1
            S.force_hard = False

        _nops = int(os.environ.get('KDBG_NOPS', '0'))
        if _nops:
            print('total ops', len(S.ops), 'truncating to', _nops, 'last line', S.ops[min(_nops, len(S.ops)) - 1]['line'])
            S.ops = S.ops[:_nops]
        run_sched(nc, S)
    return nc


def _consts():
    ident = np.eye(128, dtype=np.float32)
    triu = np.triu(np.ones((128, 128), dtype=np.float32))
    invw = np.zeros((128, 2), np.float32)
    invc = np.zeros((128, 2, 15), np.float32)
    for j in range(2):
        for p in range(128):
            w = POOL_WINDOWS[2 * j + p // 64]
            invw[p, j] = 1.0 / w
            for t in range(15):
                invc[p, j, t] = 1.0 / min(w, t + 1)
    return ident, triu, invw, invc


_PROG_CACHE = {}


def run(inputs, nseq, S_len, trace=False):
    key = (nseq, S_len)
    if key not in _PROG_CACHE:
        _PROG_CACHE[key] = build_program(nseq, S_len)
    nc = _PROG_CACHE[key]
    ident, triu, invw, invc = _consts()
    f = lambda a: np.ascontiguousarray(np.asarray(a, dtype=np.float32))
    wnames = ['w_in', 'pool_w', 'pool_scale', 'gmlp_ln_g', 'gmlp_ln_b', 'gmlp_w', 'gmlp_b', 'conv_w', 'conv_b',
              'conv_ln_g', 'conv_ln_b', 'conv_pw_w', 'conv_pw_b', 'w_out', 'b_out', 'ln_g', 'ln_b']
    shared = {k: f(inputs[k]) for k in wnames}
    shared.update(c_ident=ident, c_triu=triu, c_invw=invw, c_invc=invc)
    xp = f(inputs['x_prompt'])
    xs = f(inputs['x_sample'])
    sp_ = f(inputs['state_pool'])
    sc_ = f(inputs['state_conv'])
    in_maps = []
    for c in range(NCORES):
        m = dict(shared)
        m['xp'] = np.ascontiguousarray(xp[c * nseq:(c + 1) * nseq])
        m['xs'] = np.ascontiguousarray(xs[c])
        m['st_pool'] = np.ascontiguousarray(sp_[:, c])
        m['st_conv'] = np.ascontiguousarray(sc_[:, c])
        in_maps.append(m)
    res = run_bass_kernel_spmd(nc, in_maps, core_ids=list(range(NCORES)), **({'trace': True} if trace else {}))
    R = res.results
    y_prompt = np.concatenate([r['yp'] for r in R], axis=0)
    y_sample = np.stack([r['ys'] for r in R], axis=0)
    npp = np.concatenate([r['npp'] for r in R], axis=1)
    ncp = np.concatenate([r['ncp'] for r in R], axis=1)
    nps = np.stack([r['nps'] for r in R], axis=1)
    ncs = np.stack([r['ncs'] for r in R], axis=1)
    nvs = np.stack([r['nvs'] for r in R], axis=1)
    outs = (y_prompt, y_sample, npp, ncp, nps, ncs, nvs)
    return tuple(np.ascontiguousarray(o, dtype=np.float32) for o in outs), res


def kernel(**inputs):
    B, S_len = inputs['x_prompt'].shape[0], inputs['x_prompt'].shape[1]
    outs, _ = run(inputs, B // NCORES, S_len)
    return outs
```

```python
import contextlib
import os
import sys
import numpy as np
import concourse.bass as bass
import concourse.mybir as mybir
from concourse.bass_utils import run_bass_kernel_spmd

F32 = mybir.dt.float32
BF16 = mybir.dt.bfloat16
I32 = mybir.dt.int32
AF = mybir.ActivationFunctionType
ALU = mybir.AluOpType

D_MODEL = 1024
DEPTH = 2
D_POOL = 256
D_GMLP = 384
D_CONV = 384
D_IN = 2816
POOL_WINDOWS = (2, 4, 8, 16)
CONV_W = 31
ALPHA = float((2 * DEPTH) ** 0.25)
LN_EPS = 1e-5
NCORES = 8

COLS = dict(a=(0, 256), ga=(256, 256), u=(512, 384), v=(896, 384), gb=(1280, 384),
            c1=(1664, 384), c2=(2048, 384), gc=(2432, 384))
GROUPS = [('v', 896, 384), ('c2', 2048, 384), ('c1', 1664, 384), ('a', 0, 256), ('ga', 256, 256),
          ('gb', 1280, 384), ('u', 512, 384), ('gc', 2432, 384)]
GOFF = {}
_o = 0
for _n, _c, _w in GROUPS:
    GOFF[_n] = _o
    _o += _w
NG = len(GROUPS)
NRING = 3


class Sched:
    ENG = ('pe', 'act', 'dve', 'pool', 'sp')

    def __init__(self):
        self.ops = []
        self.force_hard = False
        self.defer = False
        self.deferred = []

    def flush(self, k=None):
        k = len(self.deferred) if k is None else min(k, len(self.deferred))
        self.ops.extend(self.deferred[:k])
        del self.deferred[:k]

    def op(self, eng, fn, reads=(), writes=(), chan=None, hard=False, partial=False):
        hard = hard or (self.force_hard and eng != 'pe')
        tgt = self.deferred if self.defer else self.ops
        tgt.append(dict(eng=eng, fn=fn, reads=tuple(reads), writes=tuple(writes), chan=chan,
                             hard=hard, partial=partial, idx=-1, line=sys._getframe(1).f_lineno))

    def analyze(self):
        assert not self.deferred
        for i, o in enumerate(self.ops):
            o['idx'] = i
        writers = {}
        readers = {}
        for o in self.ops:
            deps = set()
            for r in o['reads']:
                deps.update(writers.get(r, ()))
            for w in o['writes']:
                if not o['partial']:
                    deps.update(writers.get(w, ()))
                elif writers.get(w) and not self.ops[writers[w][0]]['partial']:
                    deps.add(writers[w][0])
                deps.update(readers.get(w, ()))
            deps.discard(o['idx'])
            o['deps'] = deps
            for r in o['reads']:
                readers.setdefault(r, []).append(o['idx'])
            for w in o['writes']:
                if o['partial']:
                    writers.setdefault(w, []).append(o['idx'])
                else:
                    writers[w] = [o['idx']]
                readers[w] = []
        need = set()
        for o in self.ops:
            for d in o['deps']:
                do = self.ops[d]
                if do['chan'] is not None or do['eng'] != o['eng'] or do['hard']:
                    need.add(d)
        cnt = {e: 0 for e in self.ENG}
        ccnt = {}
        for o in self.ops:
            if o['chan'] is not None:
                ccnt[o['chan']] = ccnt.get(o['chan'], 0) + 16
                o['sig'] = ('c', o['chan'], ccnt[o['chan']])
            elif o['idx'] in need:
                cnt[o['eng']] += 1
                o['sig'] = ('e', o['eng'], cnt[o['eng']])
            else:
                o['sig'] = None
        self.chan_total = ccnt
        self.eng_total = cnt


def run_sched(nc, S, final_wait_eng='sp'):
    S.analyze()
    with contextlib.ExitStack() as st:
        sem_e = {e: st.enter_context(nc.semaphore('s_' + e)) for e in S.ENG}
        sem_c = {c: st.enter_context(nc.semaphore('c_' + str(c))) for c in S.chan_total}
        block = st.enter_context(nc.Block())

        def semof(sig):
            return sem_e[sig[1]] if sig[0] == 'e' else sem_c[sig[1]]

        def gen(engname, engobj):
            waited = {}
            for o in S.ops:
                if o['eng'] != engname:
                    continue
                need_w = {}
                for d in o['deps']:
                    do = S.ops[d]
                    if do['chan'] is None and do['eng'] == engname and not do['hard']:
                        continue
                    sig = do['sig']
                    key = (sig[0], sig[1])
                    if sig[2] > need_w.get(key, 0):
                        need_w[key] = sig[2]
                for key in sorted(need_w):
                    if waited.get(key, 0) >= need_w[key]:
                        continue
                    engobj.wait_ge(semof((key[0], key[1], 0)), need_w[key])
                    waited[key] = need_w[key]
                ins = o['fn'](engobj)
                if o['sig'] is not None:
                    ins.then_inc(semof(o['sig']), 16 if o['sig'][0] == 'c' else 1)
            if engname == final_wait_eng:
                for c, tot in S.chan_total.items():
                    if waited.get(('c', c), 0) < tot:
                        engobj.wait_ge(sem_c[c], tot)
                for e, tot in S.eng_total.items():
                    if e != engname and tot > 0 and waited.get(('e', e), 0) < tot:
                        engobj.wait_ge(sem_e[e], tot)

        @block.tensor
        def _(e):
            gen('pe', e)

        @block.scalar
        def _(e):
            gen('act', e)

        @block.vector
        def _(e):
            gen('dve', e)

        @block.gpsimd
        def _(e):
            gen('pool', e)

        @block.sync
        def _(e):
            gen('sp', e)


def build_program(nseq, S_len, T=512):
    assert S_len % T == 0
    ntile = S_len // T
    nc = bass.Bass("TRN2", target_bir_lowering=False)
    dt_in = lambda name, shape: nc.dram_tensor(name, list(shape), F32, kind="ExternalInput").ap()
    dt_out = lambda name, shape: nc.dram_tensor(name, list(shape), F32, kind="ExternalOutput").ap()
    xp = dt_in("xp", [nseq, S_len, D_MODEL])
    xs = dt_in("xs", [16, D_MODEL])
    st_pool = dt_in("st_pool", [DEPTH, 15, D_POOL])
    st_conv = dt_in("st_conv", [DEPTH, 30, D_CONV])
    w_in = dt_in("w_in", [DEPTH, D_MODEL, D_IN])
    pool_w = dt_in("pool_w", [DEPTH, 4, 64, 64])
    pool_scale = dt_in("pool_scale", [DEPTH, D_POOL])
    gmlp_ln_g = dt_in("gmlp_ln_g", [DEPTH, D_GMLP])
    gmlp_ln_b = dt_in("gmlp_ln_b", [DEPTH, D_GMLP])
    gmlp_w = dt_in("gmlp_w", [DEPTH, 4, 128, 128])
    gmlp_b = dt_in("gmlp_b", [DEPTH, 4, 128])
    conv_w = dt_in("conv_w", [DEPTH, CONV_W, D_CONV])
    conv_b = dt_in("conv_b", [DEPTH, D_CONV])
    conv_ln_g = dt_in("conv_ln_g", [DEPTH, D_CONV])
    conv_ln_b = dt_in("conv_ln_b", [DEPTH, D_CONV])
    conv_pw_w = dt_in("conv_pw_w", [DEPTH, D_CONV, D_CONV])
    conv_pw_b = dt_in("conv_pw_b", [DEPTH, D_CONV])
    w_out = dt_in("w_out", [DEPTH, D_MODEL, D_MODEL])
    b_out = dt_in("b_out", [DEPTH, D_MODEL])
    ln_g = dt_in("ln_g", [DEPTH, D_MODEL])
    ln_b = dt_in("ln_b", [DEPTH, D_MODEL])
    c_ident = dt_in("c_ident", [128, 128])
    c_triu = dt_in("c_triu", [128, 128])
    c_invw = dt_in("c_invw", [128, 2])
    c_invc = dt_in("c_invc", [128, 2, 15])

    yp = dt_out("yp", [nseq, S_len, D_MODEL])
    ys = dt_out("ys", [16, D_MODEL])
    npp = dt_out("npp", [DEPTH, nseq, 15, D_POOL])
    ncp = dt_out("ncp", [DEPTH, nseq, 30, D_CONV])
    nps = dt_out("nps", [DEPTH, 15, D_POOL])
    ncs = dt_out("ncs", [DEPTH, 30, D_CONV])
    nvs = dt_out("nvs", [DEPTH, 16, D_GMLP])

    scr_wout = nc.dram_tensor("scr_wout", [DEPTH, 128, 9 * 1024], BF16, kind="Internal").ap()

    NS = T // 128
    AW = 16 + T
    CW = 30 + T

    with contextlib.ExitStack() as st:
        def sb(name, shape, dt=F32):
            return st.enter_context(nc.sbuf_tensor(name, list(shape), dt))

        def psm(name):
            return st.enter_context(nc.psum_tensor(name, [128, 512], F32))

        x_tok = [sb("x_tok%d" % i, [128, NS, 1024]) for i in range(2)]
        xT = sb("xT", [128, 8, T], BF16)
        mixed = sb("mixed", [128, 9, T], BF16)
        wring = [sb("wring%d" % i, [128, 8 * 384], BF16) for i in range(NRING)]
        wout_sb = sb("wout_sb", [128, 9, 1024], BF16)
        pw_sb = sb("pw_sb", [128, DEPTH, 3, 384], BF16)
        bd_sb = sb("bd_sb", [128, DEPTH, 2, 128], BF16)
        wmt_sb = sb("wmt_sb", [128, DEPTH, 4, 128], BF16)
        D_sb = sb("D_sb", [128, 3, CONV_W, 128], BF16)
        a_ext = [sb("a_ext%d" % l, [128, 2, AW]) for l in range(DEPTH)]
        c_ext = [sb("c_ext%d" % l, [128, 3, CW], BF16) for l in range(DEPTH)]
        tmpA = sb("tmpA", [128, AW])
        tmpB = sb("tmpB", [128, AW])
        dT = sb("dT", [128, 2, T], BF16)
        sga = sb("sga", [128, 2, T])
        vnf = [sb("vnf%d" % i, [128, 384]) for i in range(2)]
        vn = sb("vn", [128, NS, 384], BF16)
        tmpR = [sb("tmpR%d" % i, [128, T]) for i in range(3)]
        ug = [sb("ug%d" % i, [128, T]) for i in range(2)]
        sgb4 = sb("sgb4", [128, T])
        sgc = sb("sgc", [128, 3, T])
        yv = sb("yv", [128, 3, T])
        ysq = sb("ysq", [128, 3, T])
        mean_sb = sb("mean_sb", [128, T])
        rstd_sb = sb("rstd_sb", [128, T])
        ysil = sb("ysil", [128, 3, T], BF16)
        cf = sb("cf", [128, 3, 30])
        lnbc = sb("lnbc", [128, 2, 1024])
        glnbc = sb("glnbc", [128, 2, 384])
        ident = sb("ident", [128, 128])
        epsc = sb("epsc", [128, 1])
        lgb = sb("lgb", [128, DEPTH, 8, 2])
        triu = sb("triu", [128, 128])
        invw = sb("invw", [128, 2])
        invc = sb("invc", [128, 2, 15])
        onesf = sb("onesf", [128, 128])
        onesb = sb("onesb", [128, 128], BF16)
        cw_sb = sb("cw_sb", [128, DEPTH, 3, CONV_W])
        pscale = sb("pscale", [128, DEPTH, 2])
        cvec = sb("cvec", [128, DEPTH, 4, 3])
        hl = sb("hl", [2, 3072], BF16)
        stats = sb("stats", [128, 8, 32])
        vstats = sb("vstats", [128, 4, 16])
        sto = [sb("sto%d" % i, [32, 384]) for i in range(1)]

        banks = [psm("ps%d" % i) for i in range(8)]
        pwst = tmpA[:, 0:256].rearrange("p (l j c) -> p l j c", l=DEPTH, j=2)
        gst = tmpB[:, 0:128]
        sti = tmpR[0][0:32, 0:384]
        S = Sched()
        bank_ctr = [0]

        def nb():
            i = bank_ctr[0] % 8
            bank_ctr[0] += 1
            return banks[i], 'ps%d' % i

        stat_ctr = [0]

        def nstat():
            i = stat_ctr[0] % 8
            stat_ctr[0] += 1
            return stats[:, i:i + 1, :], 'stat%d' % i

        tr_ctr = [0]

        def ntmp():
            i = tr_ctr[0] % 3
            tr_ctr[0] += 1
            return tmpR[i], 'tmpR%d' % i

        sto_ctr = [0]

        S.op('sp', lambda e: e.dma_start(out=ident[:], in_=c_ident[:, :]), writes=['ident'], chan='k_id')
        S.op('sp', lambda e: e.dma_start(out=triu[:], in_=c_triu[:, :]), writes=['triu'], chan='k_tr')
        S.op('sp', lambda e: e.dma_start(out=invw[:], in_=c_invw[:, :]), writes=['invw'], chan='k_iw')
        S.op('sp', lambda e: e.dma_start(out=invc[:], in_=c_invc[:, :, :]), writes=['invc'], chan='k_ic')
        S.op('pool', lambda e: e.memset(onesf[:], 1.0 / 384.0), writes=['onesf'])
        S.op('pool', lambda e: e.memset(onesb[:], 1.0), writes=['onesb'])
        S.op('pool', lambda e: e.memset(epsc[:], LN_EPS), writes=['epsc'])
        S.op('pool', lambda e: e.memset(bd_sb[:], 0.0), writes=['bd'])

        def side_vec():
            def tr_rows(src_tile, nrows, col0, dst_ap_fn, rres, wres, part):
                pb, pbn = nb()
                S.op('pe', (lambda pb: lambda e: e.transpose(out=pb[:, 0:nrows], in_=src_tile[0:nrows, col0:col0 + 128], identity=ident[0:nrows, 0:nrows]))(pb),
                     reads=[rres, 'ident'], writes=[pbn])
                S.op('dve', (lambda pb: lambda e: e.tensor_copy(out=dst_ap_fn(), in_=pb[:, 0:nrows]))(pb), reads=[pbn], writes=[wres], partial=part)

            for l in range(DEPTH):
                st_t, st_n = tmpR[1 + l], 'tmpR%d' % (1 + l)
                S.op('sp', (lambda l, st_t: lambda e: e.dma_start(out=st_t[0:CONV_W, 0:384], in_=conv_w[l, :, :]))(l, st_t), writes=[st_n], chan='k_cw%d' % l)
                for cc in range(3):
                    tr_rows(st_t, CONV_W, cc * 128, (lambda l, cc: lambda: cw_sb[:, l, cc, :])(l, cc), st_n, 'cw', True)
            for vi, vec in enumerate((conv_b, conv_ln_g, conv_ln_b, conv_pw_b)):
                S.op('sp', (lambda vi, vec: lambda e: e.dma_start(out=ug[0][2 * vi:2 * vi + 2, 0:384], in_=vec[:, :]))(vi, vec), writes=['ug0'], chan='k_cv', partial=True)
            for cc in range(3):
                tr_rows(ug[0], 8, cc * 128, (lambda cc: lambda: cvec[:, :, :, cc].rearrange("p l v -> p v l"))(cc), 'ug0', 'cvec', True)
            S.op('sp', lambda e: e.dma_start(out=ug[1][0:2, 0:256], in_=pool_scale[:, :]), writes=['ug1'], chan='k_ps')
            for jc in range(2):
                tr_rows(ug[1], 2, jc * 128, (lambda jc: lambda: pscale[:, :, jc])(jc), 'ug1', 'pscale', True)
            sgaf = sga[:].rearrange("p a b -> p (a b)")
            for vi, vec in enumerate((ln_g, ln_b)):
                S.op('sp', (lambda vi, vec: lambda e: e.dma_start(out=sgaf[2 * vi:2 * vi + 2, 0:1024], in_=vec[:, :]))(vi, vec), writes=['sga'], chan='k_lg', partial=True)
            for kc in range(8):
                tr_rows(sgaf, 4, kc * 128, (lambda kc: lambda: lgb[:, :, kc, :].rearrange("p l v -> p v l"))(kc), 'sga', 'lgb', True)

        def side_pw():
            for l in range(DEPTH):
                for g in range(4):
                    j, r = g // 2, g % 2
                    S.op('sp', (lambda l, g, j, r: lambda e: e.dma_start(out=pwst[64 * r:64 * r + 64, l, j, :], in_=pool_w[l, g, :, :]))(l, g, j, r),
                         writes=['tmpA'], chan='k_pw', partial=True)
            for l in range(DEPTH):
                for g in range(4):
                    j, r = g // 2, g % 2
                    S.op('dve', (lambda l, j, r: lambda e: e.tensor_copy(out=bd_sb[64 * r:64 * r + 64, l, j, 64 * r:64 * r + 64],
                                                                         in_=pwst[64 * r:64 * r + 64, l, j, :]))(l, j, r),
                         reads=['tmpA'], writes=['bd'], partial=True)

        yvf = yv[:].rearrange("p a b -> p (a b)")
        ysqf = ysq[:].rearrange("p a b -> p (a b)")
        ysilf = ysil[:].rearrange("p a b -> p (a b)")
        YR = ['yv0', 'yv1', 'yv2']
        QR = ['ysq0', 'ysq1', 'ysq2']
        LR = ['ysil0', 'ysil1', 'ysil2']
        hl_jobs = []
        for l in range(DEPTH):
            hl_jobs.append((b_out[l, :].unsqueeze(0), 1024, l * 1024))
            hl_jobs.append((gmlp_b[l, :, :].rearrange("h t -> (h t)").unsqueeze(0), 512, 2048 + l * 512))
        def hl_job(jj):
            src, nel, c0 = hl_jobs[jj]
            S.op('sp', (lambda src, nel: lambda e: e.dma_start(out=yvf[0:1, 0:nel], in_=src))(src, nel), writes=YR, chan='k_hl')
            S.op('dve', (lambda nel, c0: lambda e: e.tensor_copy(out=hl[0:1, c0:c0 + nel], in_=yvf[0:1, 0:nel]))(nel, c0),
                 reads=YR, writes=['hl'], hard=True, partial=True)
            S.op('dve', (lambda nel, c0: lambda e: e.tensor_copy(out=ysqf[0:1, 0:nel], in_=hl[0:1, c0:c0 + nel]))(nel, c0), reads=['hl'], writes=QR, hard=True)
            S.op('dve', (lambda nel: lambda e: e.tensor_tensor(out=ysqf[0:1, 0:nel], in0=yvf[0:1, 0:nel], in1=ysqf[0:1, 0:nel], op=ALU.subtract))(nel),
                 reads=YR + QR, writes=QR, hard=True)
            S.op('dve', (lambda nel: lambda e: e.tensor_copy(out=ysilf[0:1, 0:nel], in_=ysqf[0:1, 0:nel]))(nel), reads=QR, writes=LR, hard=True)
            S.op('sp', (lambda nel, c0: lambda e: e.dma_start(out=hl[1:2, c0:c0 + nel], in_=ysilf[0:1, 0:nel]))(nel, c0), reads=LR, writes=['hl'], chan='k_hl2', partial=True)


        def gm_job(jj):
            l, h = jj // 4, jj % 4
            stg_t, stg_n = (yv, 'yv%d' % (jj % 3)) if False else (None, None)
            gsrc = sgc[:, jj // 4, (jj % 4) * 128:(jj % 4) * 128 + 128]
            gres = 'gm_stage%d' % jj
            pb, pbn = nb()
            S.op('sp', (lambda l, h, gsrc: lambda e: e.dma_start(out=gsrc, in_=gmlp_w[l, h, :, :]))(l, h, gsrc), writes=[gres, 'sgc'], chan='k_gst%d' % jj, partial=True)
            S.op('pe', (lambda pb, gsrc: lambda e: e.transpose(out=pb[:, 0:128], in_=gsrc, identity=ident[:]))(pb, gsrc),
                 reads=[gres, 'ident'], writes=[pbn])
            S.op('dve', (lambda l, h, pb: lambda e: e.tensor_tensor(out=wmt_sb[:, l, h, :], in0=pb[:, 0:128], in1=triu[:], op=ALU.mult))(l, h, pb),
                 reads=[pbn, 'triu'], writes=['wmt'], partial=True)


        ALLSCR = ['scr_wout%d' % l_ for l_ in range(DEPTH)]
        tiles = []
        for sq in range(nseq):
            for j in range(ntile):
                tiles.append(dict(kind='p', seq=sq, j=j, T=T, ns=NS, psz=128, first=(j == 0), last=(j == ntile - 1)))
        tiles.append(dict(kind='s', seq=0, j=0, T=16, ns=1, psz=16, first=False, last=True))
        ntl = len(tiles) * DEPTH
        gseq = [(n, g) for n in range(ntl) for g in range(NG)]

        def emit_group_load(q):
            if q >= len(gseq):
                return
            n, g = gseq[q]
            l = n % DEPTH
            gn, gc0, gw = GROUPS[g]
            slot = q % NRING
            S.op('pool', (lambda l, slot, gc0, gw: lambda e: e.dma_start(out=wring[slot][:, 0:8 * gw].rearrange("p (k e) -> p k e", k=8),
                                                                       in_=w_in[l, :, gc0:gc0 + gw].rearrange("(k p) e -> p k e", p=128)))(l, slot, gc0, gw),
                 writes=['wring%d' % slot], chan='ring%d' % slot)

        def emit_x_load(ti):
            if ti >= len(tiles):
                return
            tl = tiles[ti]
            slot = ti % 2
            if tl['kind'] == 'p':
                for s in range(NS):
                    r0 = tl['j'] * T + s * 128
                    S.op('sp', (lambda slot, s, sq, r0: lambda e: e.dma_start(out=x_tok[slot][:, s, :], in_=xp[sq, r0:r0 + 128, :]))(slot, s, tl['seq'], r0),
                         writes=['x_tok%d_%d' % (slot, s)], chan='x%d_%d' % (slot, s))
            else:
                S.op('sp', (lambda slot: lambda e: e.dma_start(out=x_tok[slot][0:16, 0, :], in_=xs[:, :]))(slot),
                     writes=['x_tok%d_0' % slot], chan='x%d_0' % slot)

        def emit_wout_chunk(n, k):
            l = n % DEPTH
            S.op('sp', (lambda l, k: lambda e: e.dma_start(out=wout_sb[:, k, :], in_=scr_wout[l, :, k * 1024:(k + 1) * 1024]))(l, k),
                 reads=ALLSCR, writes=['wout_sb'], chan='wout', partial=(k != 0))

        def emit_lnbc_load(n):
            if n >= ntl:
                return
            l = n % DEPTH
            S.op('sp', (lambda l: lambda e: e.dma_start(out=lnbc[:, 0, :], in_=ln_g[l, :].partition_broadcast(128)))(l), writes=['lnbc'], chan='lnbc')
            S.op('sp', (lambda l: lambda e: e.dma_start(out=lnbc[:, 1, :], in_=ln_b[l, :].partition_broadcast(128)))(l), writes=['lnbc'], chan='lnbc', partial=True)

        def emit_glnbc_load(n):
            if n >= ntl:
                return
            l = n % DEPTH
            S.op('sp', (lambda l: lambda e: e.dma_start(out=glnbc[:, 0, :], in_=gmlp_ln_g[l, :].partition_broadcast(128)))(l), writes=['glnbc'], chan='glnbc')
            S.op('sp', (lambda l: lambda e: e.dma_start(out=glnbc[:, 1, :], in_=gmlp_ln_b[l, :].partition_broadcast(128)))(l), writes=['glnbc'], chan='glnbc', partial=True)

        def emit_D_regen(n, chunks=(0, 1, 2)):
            if n >= ntl:
                return
            l = n % DEPTH
            for cc in chunks:
                S.op('pool', (lambda l, cc: lambda e: e.tensor_tensor(out=D_sb[:, cc, :, :], in0=ident[:].unsqueeze(1).to_broadcast([128, CONV_W, 128]),
                                                                      in1=cw_sb[:, l, cc, :].unsqueeze(2).to_broadcast([128, CONV_W, 128]), op=ALU.mult))(l, cc),
                     reads=['ident', 'cw'], writes=['D%d' % cc])

        def rsqrt_dve(x, y, t, rx, ry, rt, hard):
            xi, yi = x.bitcast(I32), y.bitcast(I32)
            S.op('dve', lambda e: e.tensor_scalar(out=yi, in0=xi, scalar1=1, scalar2=None, op0=ALU.arith_shift_right), reads=rx, writes=ry, hard=hard)
            S.op('dve', lambda e: e.tensor_scalar(out=yi, in0=yi, scalar1=-1, scalar2=0x5f3759df, op0=ALU.mult, op1=ALU.add), reads=ry, writes=ry, hard=hard)
            for _ in range(3):
                S.op('dve', lambda e: e.tensor_tensor(out=t, in0=y, in1=y, op=ALU.mult), reads=ry, writes=rt, hard=hard)
                S.op('dve', lambda e: e.tensor_tensor(out=t, in0=t, in1=x, op=ALU.mult), reads=rt + rx, writes=rt, hard=hard)
                S.op('dve', lambda e: e.tensor_scalar(out=t, in0=t, scalar1=-0.5, scalar2=1.5, op0=ALU.mult, op1=ALU.add), reads=rt, writes=rt, hard=hard)
                S.op('dve', lambda e: e.tensor_tensor(out=y, in0=y, in1=t, op=ALU.mult), reads=ry + rt, writes=ry, hard=hard)

        def ln_small(psz, mv, nm, ncol=1):
            S.op('act', lambda e: e.activation(out=mv[0:psz, :, 3:4], in_=mv[0:psz, :, 1:2], func=AF.Sqrt, bias=epsc[0:psz, 0:1], scale=1.0),
                 reads=[nm, 'epsc'], writes=[nm], hard=True)
            S.op('dve', lambda e: e.reciprocal(out=mv[0:psz, :, 2:3], in_=mv[0:psz, :, 3:4]), reads=[nm], writes=[nm], hard=True)

        def state_out(src_fn, nch, ncol, dst_ap, reads):
            i = 0
            sto_ctr[0] += 1
            pb, pbn = nb()
            for ch in range(nch):
                S.op('pe', (lambda ch, pb: lambda e: e.transpose(out=pb[0:ncol, ch * 128:(ch + 1) * 128], in_=src_fn(ch), identity=ident[:]))(ch, pb),
                     reads=list(reads) + ['ident'], writes=[pbn])
            S.op('act', (lambda i, pb: lambda e: e.copy(out=sto[i][0:ncol, 0:nch * 128], in_=pb[0:ncol, 0:nch * 128]))(i, pb),
                 reads=[pbn], writes=['sto%d' % i])
            S.op('sp', (lambda i: lambda e: e.dma_start(out=dst_ap, in_=sto[i][0:ncol, 0:nch * 128]))(i),
                 reads=['sto%d' % i], chan='sto%d' % i)

        def emit_transposes(ti, s, affine_l=None):
            tl = tiles[ti]
            psz = tl['psz']
            slot = ti % 2
            xt = x_tok[slot]
            xr = 'x_tok%d_%d' % (slot, s)
            for hb in range(2):
                pb, pbn = nb()
                for q in range(4):
                    kc = hb * 4 + q
                    S.op('pe', (lambda kc, q, pb: lambda e: e.transpose(out=pb[:, q * 128:q * 128 + psz], in_=xt[0:psz, s, kc * 128:(kc + 1) * 128],
                                                                        identity=ident[0:psz, 0:psz]))(kc, q, pb),
                         reads=[xr, 'ident'], writes=[pbn])
                if affine_l is None:
                    S.op('act', (lambda hb, pb: lambda e: e.copy(out=xT[:, hb * 4:hb * 4 + 4, s * 128:s * 128 + psz],
                                                                 in_=pb[:].rearrange("p (q t) -> p q t", q=4)[:, :, 0:psz]))(hb, pb),
                         reads=[pbn], writes=['xT_%d' % s])
                else:
                    for q in range(4):
                        kc = hb * 4 + q
                        S.op('act', (lambda kc, q, pb: lambda e: e.activation(out=xT[:, kc, s * 128:s * 128 + psz], in_=pb[:, q * 128:q * 128 + psz], func=AF.Identity,
                                                                              bias=lgb[:, affine_l, kc, 1:2], scale=lgb[:, affine_l, kc, 0:1]))(kc, q, pb),
                             reads=[pbn, 'lgb'], writes=['xT_%d' % s], partial=(q != 0 or hb != 0))

        emit_x_load(0)
        for s_ in range(tiles[0]['ns']):
            emit_transposes(0, s_)
        emit_glnbc_load(0)
        side_vec()
        for jj in range(8):
            gm_job(jj)
        side_pw()
        for jj in range(len(hl_jobs)):
            hl_job(jj)
        WO_ROWS = [(0, 128), (128, 128), (256, 96), (352, 96), (448, 96), (544, 96), (640, 128), (768, 128), (896, 128)]
        for l in range(DEPTH):
            for ch, (r0, rn) in enumerate(WO_ROWS):
                S.op('pool', (lambda l, ch, r0, rn: lambda e: e.dma_start(out=scr_wout[l, :, ch * 1024:(ch + 1) * 1024], in_=w_out[l, r0:r0 + 128, :]))(l, ch, r0, rn),
                     writes=['scr_wout%d' % l], chan='k_wo%d' % l, partial=True)
        for l in range(DEPTH):
            S.op('pool', (lambda l: lambda e: e.dma_start(out=pw_sb[:, l, :, :], in_=conv_pw_w[l, :, :].rearrange("(k p) e -> p k e", p=128)))(l),
                 writes=['pw_sb'], chan='k_pwc', partial=True)
        for l in range(DEPTH):
            S.op('sp', (lambda l: lambda e: e.dma_start(out=scr_wout[l, 96:98, 2048:3072], in_=hl[0:2, l * 1024:(l + 1) * 1024]))(l),
                 reads=['hl'], writes=['scr_wout%d' % l], chan='k_bo2_%d' % l)
        for q in range(T // 128):
            S.op('sp', (lambda q: lambda e: e.dma_start(out=mixed[96:98, 2, q * 128:(q + 1) * 128], in_=onesb[0:2, 0:128]))(q),
                 reads=['onesb'], writes=['mixed_b'], chan='k_ones', partial=True)
        qload = [0]
        for _ in range(NRING):
            emit_group_load(qload[0])
            qload[0] += 1
        emit_D_regen(0)

        def group_ready(q):
            return wring[q % NRING], 'wring%d' % (q % NRING)

        def tile_layer(ti, tl, l, n):
            Tt, ns, psz = tl['T'], tl['ns'], tl['psz']
            slot = ti % 2
            xt = x_tok[slot]
            xres = ['x_tok%d_%d' % (slot, s) for s in range(ns)]
            ae, ce = a_ext[l], c_ext[l]
            aen, cen = 'a_ext%d' % l, 'c_ext%d' % l
            if tl['kind'] == 'p' and tl['first']:
                S.op('pool', (lambda ae: lambda e: e.memset(ae[:, :, 0:16], 0.0))(ae), writes=[aen])
                S.op('pool', (lambda ce: lambda e: e.memset(ce[:, :, 0:30], 0.0))(ce), writes=[cen])
            elif tl['kind'] == 's':
                S.op('sp', (lambda l: lambda e: e.dma_start(out=sti[0:15, 0:256], in_=st_pool[l, :, :]))(l), writes=['tmpR0'], chan='sti')
                pb, pbn = nb()
                for jc in range(2):
                    S.op('pe', (lambda jc, pb: lambda e: e.transpose(out=pb[:, jc * 16:jc * 16 + 15], in_=sti[0:15, jc * 128:(jc + 1) * 128], identity=ident[0:15, 0:15]))(jc, pb),
                         reads=['tmpR0', 'ident'], writes=[pbn])
                S.op('dve', (lambda ae, pb: lambda e: e.tensor_copy(out=ae[:, :, 1:16], in_=pb[:, 0:32].rearrange("p (j c) -> p j c", j=2)[:, :, 0:15]))(ae, pb),
                     reads=[pbn], writes=[aen])
                S.op('sp', (lambda l: lambda e: e.dma_start(out=sti[0:30, 0:384], in_=st_conv[l, :, :]))(l), reads=[], writes=['tmpR0'], chan='sti')
                pb2, pbn2 = nb()
                for cc in range(3):
                    S.op('pe', (lambda cc, pb2: lambda e: e.transpose(out=pb2[:, cc * 32:cc * 32 + 30], in_=sti[0:30, cc * 128:(cc + 1) * 128], identity=ident[0:30, 0:30]))(cc, pb2),
                         reads=['tmpR0', 'ident'], writes=[pbn2])
                S.op('dve', (lambda ce, pb2: lambda e: e.tensor_copy(out=ce[:, :, 0:30], in_=pb2[:, 0:96].rearrange("p (j c) -> p j c", j=3)[:, :, 0:30]))(ce, pb2),
                     reads=[pbn2], writes=[cen])
                S.op('sp', (lambda l: lambda e: e.dma_start(out=ncs[l, 0:14, :], in_=st_conv[l, 16:30, :]))(l), chan='ncs_cp')
            else:
                S.op('pool', (lambda ae: lambda e: e.tensor_copy(out=ae[:, :, 1:16], in_=ae[:, :, T + 1:T + 16]))(ae), reads=[aen], writes=[aen])
                S.op('pool', (lambda ce: lambda e: e.tensor_copy(out=ce[:, :, 0:30], in_=ce[:, :, T:T + 30]))(ce), reads=[cen], writes=[cen])

            xTres = ['xT_%d' % s for s in range(ns)]

            qbase = n * NG

            def use_group(gidx):
                q = qbase + gidx
                return group_ready(q)

            def done_group(gidx):
                emit_group_load(qload[0])
                qload[0] += 1
                emit_wout_chunk(n, gidx)
                if gidx == 7:
                    emit_wout_chunk(n, 8)
                if gidx == 2 and l == DEPTH - 1:
                    emit_x_load(ti + 1)
                if gidx == 4:
                    emit_lnbc_load(n)
                if gidx == 5:
                    emit_glnbc_load(n + 1)

            wt, wres = use_group(0)
            wv = wt[:, 0:8 * 384].rearrange("p (k e) -> p k e", k=8)
            vps = []
            for s in range(ns):
                pb, pbn = nb()
                for kc in range(8):
                    S.op('pe', (lambda s, kc, pb, wv: lambda e: e.matmul(pb[0:psz, 0:384], lhsT=xT[:, kc, s * 128:s * 128 + psz], rhs=wv[:, kc, :],
                                                                         start=(kc == 0), stop=(kc == 7)))(s, kc, pb, wv),
                         reads=[xTres[s], wres], writes=[pbn])
                S.op('dve', (lambda s, pb: lambda e: e.bn_stats(out=vstats[0:psz, s, 8:14], in_=pb[0:psz, 0:384]))(s, pb), reads=[pbn], writes=['vstats'], hard=True, partial=(s != 0))
                S.op('dve', (lambda s: lambda e: e.bn_aggr(out=vstats[0:psz, s, 0:2], in_=vstats[0:psz, s, 8:14]))(s), reads=['vstats'], writes=['vstats'], hard=True, partial=True)
                vps.append((pb, pbn))
            ln_small(psz, vstats[:, 0:ns, :], 'vstats', ns)
            for s in range(ns):
                pb, pbn = vps[s]
                vf = vnf[s % 2]
                vfn = 'vnf%d' % (s % 2)
                S.op('dve', (lambda s, pb, vf: lambda e: e.tensor_scalar(out=vf[0:psz, :], in0=pb[0:psz, 0:384], scalar1=vstats[0:psz, s, 0:1], scalar2=vstats[0:psz, s, 2:3],
                                                                         op0=ALU.subtract, op1=ALU.mult))(s, pb, vf),
                     reads=[pbn, 'vstats'], writes=[vfn])
                S.op('dve', (lambda vf: lambda e: e.tensor_tensor(out=vf[0:psz, :], in0=vf[0:psz, :], in1=glnbc[0:psz, 0, :], op=ALU.mult))(vf),
                     reads=[vfn, 'glnbc'], writes=[vfn])
                if tl['kind'] == 's':
                    S.op('dve', (lambda vf: lambda e: e.tensor_tensor(out=vf[0:psz, :], in0=vf[0:psz, :], in1=glnbc[0:psz, 1, :], op=ALU.add))(vf),
                         reads=[vfn, 'glnbc'], writes=[vfn])
                    S.op('dve', (lambda s, vf: lambda e: e.tensor_copy(out=vn[0:psz, s, :], in_=vf[0:psz, :]))(s, vf), reads=[vfn], writes=['vn_%d' % s])
                    S.op('sp', (lambda l, vf: lambda e: e.dma_start(out=nvs[l, :, :], in_=vf[0:16, :]))(l, vf), reads=[vfn], chan='nvs')
                else:
                    S.op('dve', (lambda s, vf: lambda e: e.tensor_tensor(out=vn[0:psz, s, :], in0=vf[0:psz, :], in1=glnbc[0:psz, 1, :], op=ALU.add))(s, vf),
                         reads=[vfn, 'glnbc'], writes=['vn_%d' % s])
            done_group(0)

            wt2, wres2 = use_group(1)
            wc2 = wt2[:, 0:8 * 384].rearrange("p (k e) -> p k e", k=8)
            sigs = []
            for cc in range(3):
                pb, pbn = nb()
                for kc in range(8):
                    S.op('pe', (lambda cc, kc, pb, wc2: lambda e: e.matmul(pb[:, 0:Tt], lhsT=wc2[:, kc, cc * 128:(cc + 1) * 128], rhs=xT[:, kc, 0:Tt],
                                                                           start=(kc == 0), stop=(kc == 7)))(cc, kc, pb, wc2),
                         reads=xTres + [wres2], writes=[pbn])
                tt, ttn = ntmp()
                S.op('act', (lambda pb, tt: lambda e: e.activation(out=tt[:, 0:Tt], in_=pb[:, 0:Tt], func=AF.Sigmoid))(pb, tt), reads=[pbn], writes=[ttn])
                sigs.append((tt, ttn))
            done_group(1)
            wt3, wres3 = use_group(2)
            wc1 = wt3[:, 0:8 * 384].rearrange("p (k e) -> p k e", k=8)
            ntail = min(30, Tt)
            for cc in range(3):
                pb, pbn = nb()
                for kc in range(8):
                    S.op('pe', (lambda cc, kc, pb, wc1: lambda e: e.matmul(pb[:, 0:Tt], lhsT=wc1[:, kc, cc * 128:(cc + 1) * 128], rhs=xT[:, kc, 0:Tt],
                                                                           start=(kc == 0), stop=(kc == 7)))(cc, kc, pb, wc1),
                         reads=xTres + [wres3], writes=[pbn])
                tt, ttn = sigs[cc]
                S.op('dve', (lambda cc, pb, tt, ce: lambda e: e.tensor_tensor(out=ce[:, cc, 30:30 + Tt], in0=pb[:, 0:Tt], in1=tt[:, 0:Tt], op=ALU.mult))(cc, pb, tt, ce),
                     reads=[pbn, ttn], writes=[cen], partial=True)
                if tl['last']:
                    S.op('dve', (lambda cc, pb, tt: lambda e: e.tensor_tensor(out=cf[:, cc, 0:ntail], in0=pb[:, Tt - ntail:Tt], in1=tt[:, Tt - ntail:Tt], op=ALU.mult))(cc, pb, tt),
                         reads=[pbn, ttn], writes=['cf'], partial=(cc != 0))
            done_group(2)
            if tl['last']:
                if tl['kind'] == 'p':
                    state_out(lambda ch: cf[:, ch, 0:30], 3, 30, ncp[l, tl['seq'], :, :], ['cf'])
                else:
                    state_out(lambda ch: cf[:, ch, 0:16], 3, 16, ncs[l, 14:30, :], ['cf'])

            wt4, wres4 = use_group(3)
            wa = wt4[:, 0:8 * 256].rearrange("p (k e) -> p k e", k=8)
            for jc in range(2):
                pb, pbn = nb()
                for kc in range(8):
                    S.op('pe', (lambda jc, kc, pb, wa: lambda e: e.matmul(pb[:, 0:Tt], lhsT=wa[:, kc, jc * 128:(jc + 1) * 128], rhs=xT[:, kc, 0:Tt],
                                                                          start=(kc == 0), stop=(kc == 7)))(jc, kc, pb, wa),
                         reads=xTres + [wres4], writes=[pbn])
                S.op('act', (lambda jc, pb, ae: lambda e: e.copy(out=ae[:, jc, 16:16 + Tt], in_=pb[:, 0:Tt]))(jc, pb, ae), reads=[pbn], writes=[aen], partial=True)
            done_group(3)
            W = 16 + Tt
            for jc in range(2):
                a_ = ae[:, jc, :]
                nlev = 2 if jc == 0 else 4
                src = a_
                bufs = [tmpA, tmpB]
                bi = 0
                for lev in range(1, nlev + 1):
                    sh = 1 << (lev - 1)
                    lo = (1 << lev)
                    dst = bufs[bi]
                    full = (lev < nlev)
                    p0 = 0 if full else 64
                    S.op('pool', (lambda src, dst, sh, lo, p0, W: lambda e: e.tensor_tensor(out=dst[p0:128, lo:W], in0=src[p0:128, lo:W], in1=src[p0:128, lo - sh:W - sh], op=ALU.add))(src, dst, sh, lo, p0, W),
                         reads=[aen, 'tmpA', 'tmpB'], writes=['tmpA' if bi == 0 else 'tmpB'])
                    if lev == nlev - 1:
                        low_src = dst
                    src = dst
                    bi ^= 1
                hi_src = src
                for (p0, p1, sr) in ((0, 64, low_src), (64, 128, hi_src)):
                    S.op('dve', (lambda jc, p0, p1, sr, a_: lambda e: e.scalar_tensor_tensor(out=dT[p0:p1, jc, 0:Tt], in0=sr[p0:p1, 16:16 + Tt], scalar=invw[p0:p1, jc:jc + 1],
                                                                                          in1=a_[p0:p1, 16:16 + Tt], op0=ALU.mult, op1=ALU.subtract))(jc, p0, p1, sr, a_),
                         reads=[aen, 'tmpA', 'tmpB', 'invw'], writes=['dT'], partial=not (jc == 0 and p0 == 0))
                    if tl['kind'] == 'p' and tl['first']:
                        S.op('pool', (lambda jc, p0, p1, sr: lambda e: e.tensor_tensor(out=cf[p0:p1, 0, 0:15], in0=sr[p0:p1, 16:31], in1=invc[p0:p1, jc, :], op=ALU.mult))(jc, p0, p1, sr),
                             reads=['tmpA', 'tmpB', 'invc', 'cf'], writes=['cf'], hard=True)
                        S.op('pool', (lambda jc, p0, p1, a_: lambda e: e.tensor_tensor(out=dT[p0:p1, jc, 0:15], in0=cf[p0:p1, 0, 0:15], in1=a_[p0:p1, 16:31], op=ALU.subtract))(jc, p0, p1, a_),
                             reads=['cf', aen, 'dT'], writes=['dT'], partial=True)
            if tl['last']:
                if tl['kind'] == 'p':
                    state_out(lambda ch: ae[:, ch, Tt + 1:Tt + 16], 2, 15, npp[l, tl['seq'], :, :], [aen])
                else:
                    state_out(lambda ch: ae[:, ch, Tt + 1:Tt + 16], 2, 15, nps[l, :, :], [aen])
            for cc in range(3):
                pb, pbn = nb()
                for k in range(CONV_W):
                    S.op('pe', (lambda cc, k, pb, ce: lambda e: e.matmul(pb[:, 0:Tt], lhsT=D_sb[:, cc, k, :], rhs=ce[:, cc, k:k + Tt], start=(k == 0), stop=(k == CONV_W - 1)))(cc, k, pb, ce),
                         reads=[cen, 'D%d' % cc], writes=[pbn])
                S.op('act', (lambda cc, pb: lambda e: e.activation(out=yv[:, cc, 0:Tt], in_=pb[:, 0:Tt], func=AF.Identity, bias=cvec[:, l, 0, cc:cc + 1], scale=1.0))(cc, pb),
                     reads=[pbn, 'cvec'], writes=['yv%d' % cc])
                S.op('act', (lambda cc, pb: lambda e: e.activation(out=ysq[:, cc, 0:Tt], in_=pb[:, 0:Tt], func=AF.Square, bias=cvec[:, l, 0, cc:cc + 1], scale=1.0))(cc, pb),
                     reads=[pbn, 'cvec'], writes=['ysq%d' % cc])
                emit_D_regen(n + 1, (cc,))
            pm, pmn = nb()
            for cc in range(3):
                S.op('pe', (lambda cc, pm: lambda e: e.matmul(pm[:, 0:Tt], lhsT=onesf[:], rhs=yv[:, cc, 0:Tt], start=(cc == 0), stop=(cc == 2)))(cc, pm),
                     reads=['yv%d' % cc, 'onesf'], writes=[pmn])
            pq, pqn = nb()
            for cc in range(3):
                S.op('pe', (lambda cc, pq: lambda e: e.matmul(pq[:, 0:Tt], lhsT=onesf[:], rhs=ysq[:, cc, 0:Tt], start=(cc == 0), stop=(cc == 2)))(cc, pq),
                     reads=['ysq%d' % cc, 'onesf'], writes=[pqn])
            S.op('act', (lambda pm: lambda e: e.copy(out=mean_sb[:, 0:Tt], in_=pm[:, 0:Tt]))(pm), reads=[pmn], writes=['mean_sb'])
            for cc in range(3):
                S.op('pool', (lambda cc: lambda e: e.tensor_tensor(out=yv[:, cc, 0:Tt], in0=yv[:, cc, 0:Tt], in1=mean_sb[:, 0:Tt], op=ALU.subtract))(cc),
                     reads=['yv%d' % cc, 'mean_sb'], writes=['yv%d' % cc])
            cvt = ysq[:, 0, 0:Tt]
            cvt2 = ysq[:, 1, 0:Tt]
            S.op('dve', (lambda cvt: lambda e: e.tensor_tensor(out=cvt, in0=mean_sb[:, 0:Tt], in1=mean_sb[:, 0:Tt], op=ALU.mult))(cvt), reads=['mean_sb'], writes=['ysq0'])
            S.op('dve', (lambda pq, cvt: lambda e: e.tensor_tensor(out=cvt, in0=pq[:, 0:Tt], in1=cvt, op=ALU.subtract))(pq, cvt), reads=[pqn, 'ysq0'], writes=['ysq0'])
            S.op('act', (lambda cvt, cvt2: lambda e: e.activation(out=cvt2, in_=cvt, func=AF.Sqrt, bias=epsc[:, 0:1], scale=1.0))(cvt, cvt2), reads=['ysq0', 'epsc'], writes=['ysq1'])
            S.op('dve', (lambda cvt2: lambda e: e.reciprocal(out=rstd_sb[:, 0:Tt], in_=cvt2))(cvt2), reads=['ysq1'], writes=['rstd_sb'])
            wt4b, wres4b = use_group(4)
            wga = wt4b[:, 0:8 * 256].rearrange("p (k e) -> p k e", k=8)
            for jc in range(2):
                pb, pbn = nb()
                for kc in range(8):
                    S.op('pe', (lambda jc, kc, pb, wga: lambda e: e.matmul(pb[:, 0:Tt], lhsT=wga[:, kc, jc * 128:(jc + 1) * 128], rhs=xT[:, kc, 0:Tt],
                                                                           start=(kc == 0), stop=(kc == 7)))(jc, kc, pb, wga),
                         reads=xTres + [wres4b], writes=[pbn])
                S.op('act', (lambda jc, pb: lambda e: e.activation(out=sga[:, jc, 0:Tt], in_=pb[:, 0:Tt], func=AF.Silu))(jc, pb), reads=[pbn], writes=['sga'], partial=(jc != 0))
            done_group(4)
            wt5, wres5 = use_group(5)
            wgb = wt5[:, 0:8 * 384].rearrange("p (k e) -> p k e", k=8)
            sgbs = []
            for h in range(4):
                pb, pbn = nb()
                for kc in range(8):
                    S.op('pe', (lambda h, kc, pb, wgb: lambda e: e.matmul(pb[0:96, 0:Tt], lhsT=wgb[:, kc, h * 96:(h + 1) * 96], rhs=xT[:, kc, 0:Tt],
                                                                          start=(kc == 0), stop=(kc == 7)))(h, kc, pb, wgb),
                         reads=xTres + [wres5], writes=[pbn])
                if h < 3:
                    tt, ttn = ntmp()
                else:
                    tt, ttn = sgb4, 'sgb4'
                S.op('act', (lambda pb, tt: lambda e: e.activation(out=tt[0:96, 0:Tt], in_=pb[0:96, 0:Tt], func=AF.Silu))(pb, tt), reads=[pbn], writes=[ttn])
                sgbs.append((tt, ttn))
            done_group(5)
            wt6, wres6 = use_group(6)
            wu = wt6[:, 0:8 * 384].rearrange("p (k e) -> p k e", k=8)
            for h in range(4):
                pb, pbn = nb()
                for kc in range(8):
                    S.op('pe', (lambda h, kc, pb, wu: lambda e: e.matmul(pb[0:96, 0:Tt], lhsT=wu[:, kc, h * 96:(h + 1) * 96], rhs=xT[:, kc, 0:Tt],
                                                                         start=(kc == 0), stop=(kc == 7)))(h, kc, pb, wu),
                         reads=xTres + [wres6], writes=[pbn])
                tt, ttn = sgbs[h]
                ugt, ugn = ug[h % 2], 'ug%d' % (h % 2)
                S.op('dve', (lambda pb, tt, ugt: lambda e: e.tensor_tensor(out=ugt[0:96, 0:Tt], in0=pb[0:96, 0:Tt], in1=tt[0:96, 0:Tt], op=ALU.mult))(pb, tt, ugt),
                     reads=[pbn, ttn], writes=[ugn])
                pz, pzn = nb()
                o0 = 2048 + l * 512 + h * 128
                S.op('pe', (lambda pz, o0: lambda e: e.matmul(pz[0:96, 0:ns * 128].rearrange("p (s t) -> p s t", s=ns)[:, :, 0:psz], lhsT=onesb[0:2, 0:96],
                                                              rhs=hl[0:2, o0:o0 + psz].unsqueeze(1).to_broadcast([2, ns, psz]), start=True, stop=False))(pz, o0),
                     reads=['onesb', 'hl'], writes=[pzn])
                for s in range(ns):
                    S.op('pe', (lambda h, s, pz: lambda e: e.matmul(pz[0:96, s * 128:s * 128 + psz], lhsT=vn[0:psz, s, h * 96:(h + 1) * 96], rhs=wmt_sb[0:psz, l, h, 0:psz],
                                                                    start=False, stop=(s == ns - 1)))(h, s, pz),
                         reads=['vn_%d' % s, 'wmt'], writes=[pzn])
                S.op('dve', (lambda h, pz, ugt: lambda e: e.tensor_tensor(out=mixed[0:96, 2 + h, 0:Tt], in0=pz[0:96, 0:Tt], in1=ugt[0:96, 0:Tt], op=ALU.mult))(h, pz, ugt),
                     reads=[pzn, ugn], writes=['mixed_b'], partial=(h != 0))
                S.flush(3)
            done_group(6)

            wt7, wres7 = use_group(7)
            wgc = wt7[:, 0:8 * 384].rearrange("p (k e) -> p k e", k=8)
            for cc in range(3):
                pb, pbn = nb()
                for kc in range(8):
                    S.op('pe', (lambda cc, kc, pb, wgc: lambda e: e.matmul(pb[:, 0:Tt], lhsT=wgc[:, kc, cc * 128:(cc + 1) * 128], rhs=xT[:, kc, 0:Tt],
                                                                           start=(kc == 0), stop=(kc == 7)))(cc, kc, pb, wgc),
                         reads=xTres + [wres7], writes=[pbn])
                S.op('act', (lambda cc, pb: lambda e: e.activation(out=sgc[:, cc, 0:Tt], in_=pb[:, 0:Tt], func=AF.Silu))(cc, pb), reads=[pbn], writes=['sgc'], partial=(cc != 0))
                S.flush(2)
            done_group(7)

            S.flush()
            for cc in range(3):
                S.op('dve', (lambda cc: lambda e: e.tensor_tensor(out=yv[:, cc, 0:Tt], in0=yv[:, cc, 0:Tt], in1=rstd_sb[:, 0:Tt], op=ALU.mult))(cc),
                     reads=['yv%d' % cc, 'rstd_sb'], writes=['yv%d' % cc])
                S.op('act', (lambda cc: lambda e: e.activation(out=ysil[:, cc, 0:Tt], in_=yv[:, cc, 0:Tt], func=AF.Silu, bias=cvec[:, l, 2, cc:cc + 1], scale=cvec[:, l, 1, cc:cc + 1]))(cc),
                     reads=['yv%d' % cc, 'cvec'], writes=['ysil%d' % cc])
            for jc in range(2):
                pb, pbn = nb()
                S.op('pe', (lambda jc, pb: lambda e: e.matmul(pb[:, 0:Tt], lhsT=bd_sb[:, l, jc, :], rhs=dT[:, jc, 0:Tt], start=True, stop=True))(jc, pb),
                     reads=['dT', 'bd'], writes=[pbn])
                S.op('dve', (lambda jc, pb: lambda e: e.scalar_tensor_tensor(out=mixed[:, jc, 0:Tt], in0=pb[:, 0:Tt], scalar=pscale[:, l, jc:jc + 1], in1=sga[:, jc, 0:Tt],
                                                                             op0=ALU.mult, op1=ALU.mult))(jc, pb),
                     reads=[pbn, 'pscale', 'sga'], writes=['mixed_a'], partial=(jc != 0))

            for eo in range(3):
                pb, pbn = nb()
                for kc in range(3):
                    S.op('pe', (lambda eo, kc, pb: lambda e: e.matmul(pb[:, 0:Tt], lhsT=pw_sb[:, l, kc, eo * 128:(eo + 1) * 128], rhs=ysil[:, kc, 0:Tt], start=(kc == 0), stop=(kc == 2)))(eo, kc, pb),
                         reads=['ysil%d' % kc, 'pw_sb'], writes=[pbn])
                S.op('dve', (lambda eo, pb: lambda e: e.scalar_tensor_tensor(out=mixed[:, 6 + eo, 0:Tt], in0=pb[:, 0:Tt], scalar=cvec[:, l, 3, eo:eo + 1], in1=sgc[:, eo, 0:Tt],
                                                                             op0=ALU.add, op1=ALU.mult))(eo, pb),
                     reads=[pbn, 'cvec', 'sgc'], writes=['mixed_c'], partial=(eo != 0))
            pending = []

            def pool_affine(s):
                S.op('pool', (lambda s: lambda e: e.tensor_tensor(out=xt[0:psz, s, :], in0=xt[0:psz, s, :], in1=lnbc[0:psz, 0, :], op=ALU.mult))(s),
                     reads=[xres[s], 'lnbc'], writes=[xres[s]])
                S.op('pool', (lambda s: lambda e: e.tensor_tensor(out=xt[0:psz, s, :], in0=xt[0:psz, s, :], in1=lnbc[0:psz, 1, :], op=ALU.add))(s),
                     reads=[xres[s], 'lnbc'], writes=[xres[s]])

            KP = [128, 128, 98, 96, 96, 96, 128, 128, 128]
            mres = ['mixed_a', 'mixed_b', 'mixed_c']
            for s in range(ns):
                pos = []
                for hf in range(2):
                    pb, pbn = nb()
                    for kc in range(9):
                        kp = KP[kc]
                        S.op('pe', (lambda s, hf, kc, kp, pb: lambda e: e.matmul(pb[0:psz, 0:512], lhsT=mixed[0:kp, kc, s * 128:s * 128 + psz], rhs=wout_sb[0:kp, kc, hf * 512:(hf + 1) * 512],
                                                                                  start=(kc == 0), stop=(kc == 8)))(s, hf, kc, kp, pb),
                             reads=mres + ['wout_sb'], writes=[pbn])
                    pos.append((pb, pbn))
                stt, stn = nstat()
                for hf in range(2):
                    pb, pbn = pos[hf]
                    S.op('dve', (lambda s, hf, pb: lambda e: e.scalar_tensor_tensor(out=xt[0:psz, s, hf * 512:(hf + 1) * 512], in0=xt[0:psz, s, hf * 512:(hf + 1) * 512], scalar=ALPHA,
                                                                                    in1=pb[0:psz, 0:512], op0=ALU.mult, op1=ALU.add))(s, hf, pb),
                         reads=[pbn, xres[s]], writes=[xres[s]])
                    S.op('dve', (lambda s, hf, stt: lambda e: e.bn_stats(out=stt[0:psz, 0, 8 + 6 * hf:14 + 6 * hf], in_=xt[0:psz, s, hf * 512:(hf + 1) * 512]))(s, hf, stt),
                         reads=[xres[s]], writes=[stn], hard=True, partial=(hf == 1))
                S.op('dve', (lambda stt: lambda e: e.bn_aggr(out=stt[0:psz, 0, 0:2], in_=stt[0:psz, 0, 8:20]))(stt), reads=[stn], writes=[stn], hard=True)
                ln_small(psz, stt, stn)
                S.op('dve', (lambda stt: lambda e: e.tensor_scalar(out=stt[0:psz, 0, 5:6], in0=stt[0:psz, 0, 0:1], scalar1=stt[0:psz, 0, 2:3], scalar2=-1.0,
                                                                   op0=ALU.mult, op1=ALU.mult))(stt), reads=[stn], writes=[stn], hard=True)
                S.op('act', (lambda s, stt: lambda e: e.activation(out=xt[0:psz, s, :], in_=xt[0:psz, s, :], func=AF.Identity, bias=stt[0:psz, 0, 5:6], scale=stt[0:psz, 0, 2:3]))(s, stt),
                     reads=[xres[s], stn], writes=[xres[s]])
                if l == DEPTH - 1:
                    pool_affine(s)
                else:
                    pending.append(s)
                    if len(pending) > 1:
                        s0 = pending.pop(0)
                        emit_transposes(ti, s0, l)
                        pool_affine(s0)
                if l == DEPTH - 1:
                    if tl['kind'] == 'p':
                        r0 = tl['j'] * T + s * 128
                        S.op('pool', (lambda s, sq, r0: lambda e: e.dma_start(out=yp[sq, r0:r0 + 128, :], in_=xt[:, s, :]))(s, tl['seq'], r0),
                             reads=[xres[s]], chan='yo%d_%d' % (slot, s))
                    else:
                        S.op('sp', lambda e: e.dma_start(out=ys[:, :], in_=xt[0:16, 0, :]), reads=[xres[0]], chan='ys_out')
            for s0 in pending:
                emit_transposes(ti, s0, l)
                pool_affine(s0)
            if l == DEPTH - 1 and ti + 1 < len(tiles):
                hard_save = S.force_hard
                S.force_hard = (tiles[ti + 1]['kind'] == 's')
                for s_ in range(tiles[ti + 1]['ns']):
                    emit_transposes(ti + 1, s_)
                S.force_hard = hard_save

        n_tl = 0
        for ti, tl in enumerate(tiles):
            S.force_hard = (tl['kind'] == 's')
            for l in range(DEPTH):
                tile_layer(ti, tl, l, n_tl)
                n_tl += 1
            S.force_hard = False

        _nops = int(os.environ.get('KDBG_NOPS', '0'))
        if _nops:
            print('total ops', len(S.ops), 'truncating to', _nops, 'last line', S.ops[min(_nops, len(S.ops)) - 1]['line'])
            S.ops = S.ops[:_nops]
        run_sched(nc, S)
    return nc


def _consts():
    ident = np.eye(128, dtype=np.float32)
    triu = np.triu(np.ones((128, 128), dtype=np.float32))
    invw = np.zeros((128, 2), np.float32)
    invc = np.zeros((128, 2, 15), np.float32)
    for j in range(2):
        for p in range(128):
            w = POOL_WINDOWS[2 * j + p // 64]
            invw[p, j] = 1.0 / w
            for t in range(15):
                invc[p, j, t] = 1.0 / min(w, t + 1)
    return ident, triu, invw, invc


_PROG_CACHE = {}


def run(inputs, nseq, S_len, trace=False):
    key = (nseq, S_len)
    if key not in _PROG_CACHE:
        _PROG_CACHE[key] = build_program(nseq, S_len)
    nc = _PROG_CACHE[key]
    ident, triu, invw, invc = _consts()
    f = lambda a: np.ascontiguousarray(np.asarray(a, dtype=np.float32))
    wnames = ['w_in', 'pool_w', 'pool_scale', 'gmlp_ln_g', 'gmlp_ln_b', 'gmlp_w', 'gmlp_b', 'conv_w', 'conv_b',
              'conv_ln_g', 'conv_ln_b', 'conv_pw_w', 'conv_pw_b', 'w_out', 'b_out', 'ln_g', 'ln_b']
    shared = {k: f(inputs[k]) for k in wnames}
    shared.update(c_ident=ident, c_triu=triu, c_invw=invw, c_invc=invc)
    xp = f(inputs['x_prompt'])
    xs = f(inputs['x_sample'])
    sp_ = f(inputs['state_pool'])
    sc_ = f(inputs['state_conv'])
    in_maps = []
    for c in range(NCORES):
        m = dict(shared)
        m['xp'] = np.ascontiguousarray(xp[c * nseq:(c + 1) * nseq])
        m['xs'] = np.ascontiguousarray(xs[c])
        m['st_pool'] = np.ascontiguousarray(sp_[:, c])
        m['st_conv'] = np.ascontiguousarray(sc_[:, c])
        in_maps.append(m)
    res = run_bass_kernel_spmd(nc, in_maps, core_ids=list(range(NCORES)), **({'trace': True} if trace else {}))
    R = res.results
    y_prompt = np.concatenate([r['yp'] for r in R], axis=0)
    y_sample = np.stack([r['ys'] for r in R], axis=0)
    npp = np.concatenate([r['npp'] for r in R], axis=1)
    ncp = np.concatenate([r['ncp'] for r in R], axis=1)
    nps = np.stack([r['nps'] for r in R], axis=1)
    ncs = np.stack([r['ncs'] for r in R], axis=1)
    nvs = np.stack([r['nvs'] for r in R], axis=1)
    outs = (y_prompt, y_sample, npp, ncp, nps, ncs, nvs)
    return tuple(np.ascontiguousarray(o, dtype=np.float32) for o in outs), res


def kernel(**inputs):
    B, S_len = inputs['x_prompt'].shape[0], inputs['x_prompt'].shape[1]
    outs, _ = run(inputs, B // NCORES, S_len)
    return outs
```
